# Optimizing a Trainium2 kernel written in Bass

```python
import jax, jax.numpy as jnp
from jax import lax
import numpy as np

D_MODEL = 1024
BATCH = 4
SEQ = 4096
DEPTH = 1
DEC_BATCH = 16
DEC_SEQ = 4096
PAST_LEN = 128

HEAD_DIM = 64
N_HEADS = 8
N_KV_HEADS = 2
Q_PER_KV = N_HEADS // N_KV_HEADS
ATTN_WIDTH = N_HEADS * HEAD_DIM
KV_WIDTH = N_KV_HEADS * HEAD_DIM
N_FGROUPS = 8
FGROUP_DIM = 64
F_WIDTH = N_FGROUPS * FGROUP_DIM
MIX_WIDTH = ATTN_WIDTH + F_WIDTH
IN_WIDTH = ATTN_WIDTH + 2 * KV_WIDTH + ATTN_WIDTH + F_WIDTH + F_WIDTH
PLE_DIM = 256
GRID_W = 64
Q_BLOCK = 128
ROPE_THETA = 10000.0
EPS = 1e-6

kernel_name = "hybrid_gqa_fnet_encoder"


def _rmsnorm(x, g):
    xf = x.astype(jnp.float32)
    xf = xf * lax.rsqrt(jnp.mean(xf * xf, axis=-1, keepdims=True) + EPS)
    return (xf * g.astype(jnp.float32)).astype(x.dtype)


def _axial_rope_tables(seq_len, dtype):
    rows = seq_len // GRID_W
    row = jnp.repeat(jnp.arange(rows, dtype=jnp.float32), GRID_W)
    col = jnp.tile(jnp.arange(GRID_W, dtype=jnp.float32), rows)
    half = HEAD_DIM // 2
    n_freq = half // 2
    inv_freq = ROPE_THETA ** (-jnp.arange(n_freq, dtype=jnp.float32) / n_freq)
    ang_r = row[:, None] * inv_freq[None, :]
    ang_c = col[:, None] * inv_freq[None, :]
    ang = jnp.concatenate([ang_r, ang_r, ang_c, ang_c], axis=-1)
    return jnp.cos(ang).astype(dtype), jnp.sin(ang).astype(dtype)


def _apply_axial_rope(x, cos, sin):
    x1, x2, x3, x4 = jnp.split(x, 4, axis=-1)
    rot = jnp.concatenate([-x2, x1, -x4, x3], axis=-1)
    return x * cos[None, :, None, :] + rot * sin[None, :, None, :]


def _blocked_gqa(q, k, v):
    B, S, _, _ = q.shape
    n_blk = S // Q_BLOCK
    scale = HEAD_DIM ** -0.5
    qb = q.reshape(B, n_blk, Q_BLOCK, N_KV_HEADS, Q_PER_KV, HEAD_DIM).transpose(1, 0, 3, 4, 2, 5)

    def one_block(qblk):
        s = jnp.einsum('bkgqd,bskd->bkgqs', qblk, k).astype(jnp.float32) * scale
        pr = jax.nn.softmax(s, axis=-1).astype(v.dtype)
        return jnp.einsum('bkgqs,bskd->bqkgd', pr, v)

    o = lax.map(one_block, qb)
    return o.transpose(1, 0, 2, 3, 4, 5).reshape(B, S, ATTN_WIDTH)


def _fourier_branch(f, w_fmix):
    B, S, _ = f.shape
    fg = f.reshape(B, S, N_FGROUPS, FGROUP_DIM).astype(jnp.float32)
    ff = jnp.real(jnp.fft.fft2(fg, axes=(1, 3), norm="ortho")).astype(f.dtype)
    fm = jnp.einsum('bsgc,gcd->bsgd', ff, w_fmix)
    return fm.reshape(B, S, F_WIDTH)


def _layer(h, p, g_norm, w_in, g_q, g_k, w_fmix, w_out, g_ple, w_ple_gate, w_ple):
    B, S, _ = h.shape
    u = _rmsnorm(h, g_norm)
    z = u @ w_in
    cuts = np.cumsum([ATTN_WIDTH, KV_WIDTH, KV_WIDTH, ATTN_WIDTH, F_WIDTH])
    q, k, v, ga, f, gf = jnp.split(z, [int(c) for c in cuts], axis=-1)
    q = q.reshape(B, S, N_HEADS, HEAD_DIM)
    k = k.reshape(B, S, N_KV_HEADS, HEAD_DIM)
    v = v.reshape(B, S, N_KV_HEADS, HEAD_DIM)
    cos, sin = _axial_rope_tables(S, h.dtype)
    q = _apply_axial_rope(_rmsnorm(q, g_q), cos, sin)
    k = _apply_axial_rope(_rmsnorm(k, g_k), cos, sin)
    a = _blocked_gqa(q, k, v) * jax.nn.silu(ga)
    fo = _fourier_branch(f, w_fmix) * jax.nn.silu(gf)
    h = h + jnp.concatenate([a, fo], axis=-1) @ w_out
    gate = jax.nn.sigmoid(_rmsnorm(h, g_ple) @ w_ple_gate)
    return h + (p @ w_ple) * gate


def _trunk(x, p, g_norm, w_in, g_q, g_k, w_fmix, w_out, g_ple, w_ple_gate, w_ple, g_final):
    h = x
    for i in range(DEPTH):
        h = _layer(h, p[i], g_norm[i], w_in[i], g_q[i], g_k[i], w_fmix[i], w_out[i],
                   g_ple[i], w_ple_gate[i], w_ple[i])
    return _rmsnorm(h, g_final)


def setup_inputs(seed: int = 0) -> dict:
    key = jax.random.key(seed)
    ks = jax.random.split(key, 16)
    f32 = jnp.float32
    nrm = lambda k, shape, s: jax.random.normal(k, shape, f32) * s
    return {
        "x_prompt": nrm(ks[0], (BATCH, SEQ, D_MODEL), 1.0),
        "x_sample": nrm(ks[1], (DEC_BATCH, DEC_SEQ, D_MODEL), 1.0),
        "p_prompt": nrm(ks[2], (DEPTH, BATCH, SEQ, PLE_DIM), 1.0),
        "p_sample": nrm(ks[3], (DEPTH, DEC_BATCH, DEC_SEQ, PLE_DIM), 1.0),
        "g_norm": 1.0 + nrm(ks[4], (DEPTH, D_MODEL), 0.02),
        "w_in": nrm(ks[5], (DEPTH, D_MODEL, IN_WIDTH), D_MODEL ** -0.5),
        "g_q": 1.0 + nrm(ks[6], (DEPTH, HEAD_DIM), 0.02),
        "g_k": 1.0 + nrm(ks[7], (DEPTH, HEAD_DIM), 0.02),
        "w_fmix": nrm(ks[8], (DEPTH, N_FGROUPS, FGROUP_DIM, FGROUP_DIM), FGROUP_DIM ** -0.5),
        "w_out": nrm(ks[9], (DEPTH, MIX_WIDTH, D_MODEL), MIX_WIDTH ** -0.5),
        "g_ple": 1.0 + nrm(ks[10], (DEPTH, D_MODEL), 0.02),
        "w_ple_gate": nrm(ks[11], (DEPTH, D_MODEL, D_MODEL), D_MODEL ** -0.5),
        "w_ple": nrm(ks[12], (DEPTH, PLE_DIM, D_MODEL), PLE_DIM ** -0.5),
        "g_final": 1.0 + nrm(ks[13], (D_MODEL,), 0.02),
    }


def reference(x_prompt, x_sample, p_prompt, p_sample, g_norm, w_in, g_q, g_k, w_fmix, w_out,
              g_ple, w_ple_gate, w_ple, g_final):
    y_prompt = _trunk(x_prompt, p_prompt, g_norm, w_in, g_q, g_k, w_fmix, w_out,
                      g_ple, w_ple_gate, w_ple, g_final)
    y_sample = _trunk(x_sample, p_sample, g_norm, w_in, g_q, g_k, w_fmix, w_out,
                      g_ple, w_ple_gate, w_ple, g_final)
    return (y_prompt, y_sample)
```

```python
import numpy as np
import ml_dtypes
from contextlib import ExitStack
import concourse.bass as bass
import concourse.mybir as mybir
from concourse.bass_utils import run_bass_kernel_spmd

F32 = mybir.dt.float32
BF16 = mybir.dt.bfloat16
AF = mybir.ActivationFunctionType
ALU = mybir.AluOpType

NCORES = 8
S = 4096
D = 1024
NT = 32
NQT = 20
EPS = 1e-6
SLOT_QT = [8, 8, 4]


class Eng:
    def __init__(self, e, sem, name):
        self.e = e
        self.sem = sem
        self.name = name
        self.count = 0
        self.waited = {}

    def wait(self, *toks):
        for tok in toks:
            if tok is None:
                continue
            if isinstance(tok, list):
                self.wait(*tok)
                continue
            key, sem, cnt = tok
            if self.waited.get(key, 0) >= cnt:
                continue
            self.e.wait_ge(sem, cnt)
            self.waited[key] = cnt

    def done(self, inst):
        self.count += 1
        inst.then_inc(self.sem, 1)
        return (self.name, self.sem, self.count)

    def last(self):
        if self.count == 0:
            return None
        return (self.name, self.sem, self.count)


def build_nc(n_slots=3, slot_qt=SLOT_QT, overlap=True):
    nc = bass.Bass("TRN2", target_bir_lowering=False)
    dt_in = lambda n, s, d=F32: nc.dram_tensor(n, s, d, kind="ExternalInput").ap()
    xs = dt_in("xs", [3, NT, 128, D])
    xq = dt_in("xq", [NQT, 512, D])
    pq = dt_in("pq", [NQT, 512, 256])
    w1 = dt_in("w1", [8, 128, 384])
    wfT = dt_in("wfT", [64, 8, D])
    w2 = dt_in("w2", [8, 128, 2048])
    wo = dt_in("wo", [8, 128, D])
    wpg = dt_in("wpg", [8, 128, D])
    wple = dt_in("wple", [2, 128, D])
    wfm = dt_in("wfm", [64, 8, 64])
    gvec = dt_in("gvec", [128, 24])
    gfin = dt_in("gfin", [128, D])
    cst32 = dt_in("cst32", [128, 384])
    dft3 = dt_in("dft3", [128, 384], BF16)
    dft1 = dt_in("dft1", [2, NT, 128, 384], BF16)
    ropek = dt_in("ropek", [NT, 128, 256])
    ropeq = dt_in("ropeq", [NQT, 128, 1024])
    y = nc.dram_tensor("y", [NQT, 512, D], F32, kind="ExternalOutput").ap()
    ysc = nc.dram_tensor("ysc", [3, 2, 32, 128, 512], BF16, kind="ExternalOutput").ap()

    with ExitStack() as es:
        sb = lambda n, s, d: es.enter_context(nc.sbuf_tensor(n, s, d))
        W1 = sb("W1", [128, 8, 384], BF16)
        W1F = sb("W1F", [128, 8, 1024], BF16)
        W2 = sb("W2", [128, 8, 2048], BF16)
        WO = sb("WO", [128, 8, 1024], BF16)
        WPG = sb("WPG", [128, 8, 1024], BF16)
        WPLE = sb("WPLE", [128, 2, 1024], BF16)
        KT = sb("KT", [128, S], BF16)
        VA = sb("VA", [128, NT, 2, 128], BF16)
        GFIN = sb("GFIN", [128, D], F32)
        GV = sb("GV", [128, 24], F32)
        C32 = sb("C32", [128, 384], F32)
        D3 = sb("D3", [128, 384], BF16)
        EPSC = sb("EPSC", [128, 4], F32)
        SM = sb("SM", [128, 256], F32)
        XS = sb("XS", [128, 2, D], F32)
        HT = sb("HT", [128, 2, D], F32)
        U32 = sb("U32", [128, D], F32)
        UTf = sb("UTf", [128, 8, 512], BF16)
        YS = UTf
        U2T = sb("U2T", [128, 8, 128], BF16)
        QT = sb("QT", [128, 2, 4, 512], BF16)
        GM = sb("GM", [128, 2, 8, 512], BF16)
        PT = sb("PT", [128, 3, 1024], BF16)
        RQ = sb("RQ", [128, 1024], F32)
        SQ = sb("SQ", [128, 512], F32)
        RS = sb("RS", [128, 512], F32)
        T1 = sb("T1", [128, 512], F32)
        T2 = sb("T2", [128, 512], F32)
        EG = sb("EG", [128, 512], F32)
        RSa = sb("RSa", [128, 512], F32)
        OAa = sb("OAa", [128, 512], F32)
        ET = sb("ET", [128, 512], F32)
        T2t = sb("T2t", [128, 512], F32)
        D1 = sb("D1", [128, 2, 384], BF16)
        PTL = sb("PTL", [128, 256], F32)
        PTT = sb("PTT", [128, 2, 128], BF16)
        PS = [es.enter_context(nc.psum_tensor(f"ps{i}", [128, 1024], F32)) for i in range(4)]
        sems = {n: es.enter_context(nc.semaphore(n)) for n in ["pe", "act", "dve", "pool"]}
        dsems = {}
        dcnt = {}

        def dsem(key):
            if key not in dsems:
                dsems[key] = es.enter_context(nc.semaphore("d_" + key))
                dcnt[key] = 0
            return dsems[key]

        es.enter_context(nc.Block())
        PE = Eng(nc.tensor, sems["pe"], "pe")
        ACT = Eng(nc.scalar, sems["act"], "act")
        DVE = Eng(nc.vector, sems["dve"], "dve")
        POOL = Eng(nc.gpsimd, sems["pool"], "pool")
        SP = Eng(nc.sync, None, "sp")
        COMPUTE = [PE, ACT, DVE, POOL]

        def dma(out, in_, key, waits=()):
            SP.wait(*waits)
            sem = dsem(key)
            nc.sync.dma_start(out=out, in_=in_).then_inc(sem, 16)
            dcnt[key] += 16
            return (key, sem, dcnt[key])

        def dtok(key):
            return (key, dsems[key], dcnt[key])

        def barrier(with_dma_keys=()):
            toks = [e.last() for e in COMPUTE]
            toks += [dtok(k) for k in with_dma_keys if k in dsems]
            for e in COMPUTE + [SP]:
                e.wait(*toks)

        def bank(b):
            return PS[b // 2][:, (b % 2) * 512:(b % 2) * 512 + 512]

        def bankb(b):
            return PS[b // 2][:, :].bitcast(BF16)[:, (b % 2) * 1024:(b % 2) * 1024 + 1024]

        bank_free = [None] * 8
        fbs = [6]

        p1_mode = [False]

        def fbank(owner=None):
            if p1_mode[0] and owner == "A":
                return 6
            if p1_mode[0] and owner == "tail":
                return 7
            b = fbs[0]
            fbs[0] = 13 - b
            return b

        smc = [0]

        def smcols(n):
            if smc[0] + n > 256:
                smc[0] = 0
            a = smc[0]
            smc[0] += n
            return SM[:, a:a + n]

        bdones = C32[:, 0:128]
        ccs = C32[0:64, 256:384]
        ident32 = C32[:, 128:256]
        identb = D3[:, 256:384]
        EPSA = EPSC[:, 0:1]
        MHALF = EPSC[:, 1:2]
        ONE = EPSC[:, 2:3]

        def delay(n):
            for _ in range(n):
                yield

        def run(gen):
            if gen is None:
                return
            for _ in gen:
                pass

        dma(GV[:], gvec[:, :], "c0")
        dma(GFIN[:], gfin[:, :], "c0")
        dma(C32[:], cst32[:, :], "c0")
        dma(D3[:], dft3[:, :], "c0")
        t_c = dtok("c0")
        nc.vector.memset(EPSC[:, 0:1], EPS)
        nc.vector.memset(EPSC[:, 1:2], -0.5)
        d_eps = DVE.done(nc.vector.memset(EPSC[:, 2:3], 1.0))
        p_ones = POOL.done(nc.gpsimd.memset(VA[:], 1.0))
        for e in COMPUTE:
            e.wait(t_c, d_eps, p_ones)
        stage = [XS[:, :, :].rearrange("p a c -> p (a c)"), HT[:, :, :].rearrange("p a c -> p (a c)")]
        stage_free = [None, None]
        sidx = [0]

        def prep(dst_ap, src_ap, ncols, gcol):
            i = sidx[0] % 2
            sidx[0] += 1
            st = stage[i][:, 0:ncols]
            t = dma(st, src_ap, f"st{i}", waits=[stage_free[i]])
            eng, e = (DVE, nc.vector)
            eng.wait(t)
            if gcol is None:
                d = eng.done(e.tensor_copy(out=dst_ap, in_=st))
            else:
                d = eng.done(e.tensor_scalar(out=dst_ap, in0=st, scalar1=gcol, scalar2=None, op0=ALU.mult))
            stage_free[i] = d
            return d

        for k in range(8):
            prep(W1[:, k, :], w1[k], 384, GV[:, k:k + 1])

        def prep_rest_g():
            for k in range(8):
                sidx[0] = 1
                prep(W2[:, k, :], w2[k], 2048, GV[:, k:k + 1])
                yield
                sidx[0] = 1
                prep(WO[:, k, :], wo[k], 1024, None)
                yield
                sidx[0] = 1
                prep(WPG[:, k, :], wpg[k], 1024, GV[:, 8 + k:9 + k])
                yield
            for k in range(2):
                sidx[0] = 1
                prep(WPLE[:, k, :], wple[k], 1024, 0.5)
                yield
        WFM = T1[0:64, :]
        t = dma(WFM, wfm.rearrange("c g d -> c (g d)"), "c1")
        PE.wait(t)
        for g in range(8):
            nc.tensor.matmul(PS[0][0:64, g * 128:g * 128 + 64], lhsT=ccs[:, 0:64], rhs=WFM[:, g * 64:(g + 1) * 64], start=True, stop=True)
            m = PE.done(nc.tensor.matmul(PS[0][0:64, g * 128 + 64:g * 128 + 128], lhsT=ccs[:, 64:128], rhs=WFM[:, g * 64:(g + 1) * 64], start=True, stop=True))
        DVE.wait(m)
        PRs = [T2[0:64, :], SQ[0:64, :]]
        DVE.done(nc.vector.tensor_copy(out=PRs[0], in_=PS[0][0:64, 0:512]))
        dpr = DVE.done(nc.vector.tensor_copy(out=PRs[1], in_=PS[0][0:64, 512:1024]))
        WFT = GM[:, :, :, :].rearrange("p a c n -> p (a c n)").bitcast(F32)[0:64, :]
        for hh in range(2):
            t = dma(WFT, wfT[:, hh * 4:(hh + 1) * 4, :].rearrange("c g f -> c (g f)"), "c2", waits=[PE.last()])
            PE.wait(t, dpr)
            for k in range(8):
                PE.wait(bank_free[2 + (k % 2)])
                for gg in range(4):
                    g = hh * 4 + gg
                    m = PE.done(nc.tensor.matmul(bank(2 + (k % 2))[:, gg * 128:(gg + 1) * 128],
                                                 lhsT=WFT[:, gg * 1024 + k * 128:gg * 1024 + (k + 1) * 128],
                                                 rhs=PRs[g // 4][:, (g % 4) * 128:(g % 4 + 1) * 128], start=True, stop=True))
                DVE.wait(m)
                src = bank(2 + (k % 2)).rearrange("p (g r d) -> p g r d", g=4, r=2)
                dst = W1F[:, k, :].rearrange("p (r g d) -> p g r d", r=2, g=8)[:, hh * 4:(hh + 1) * 4, :, :]
                d = DVE.done(nc.vector.tensor_scalar(out=dst, in0=src, scalar1=GV[:, k:k + 1], scalar2=None, op0=ALU.mult))
                bank_free[2 + (k % 2)] = d
        barrier(with_dma_keys=["c2", "st0", "st1"])

        u32_free = [None]

        def rstd_from_stats(x_ap, x_tok):
            st = smcols(12)
            mv = smcols(2)
            ms = smcols(1)
            ms2 = smcols(1)
            rstd = smcols(1)
            DVE.wait(x_tok)
            nc.vector.bn_stats(out=st[:, 0:6], in_=x_ap[:, 0:512])
            d = DVE.done(nc.vector.bn_stats(out=st[:, 6:12], in_=x_ap[:, 512:1024]))
            DVE.wait(d)
            d = DVE.done(nc.vector.bn_aggr(out=mv, in_=st))
            DVE.wait(d)
            d = DVE.done(nc.vector.scalar_tensor_tensor(out=ms, in0=mv[:, 0:1], scalar=mv[:, 0:1], in1=mv[:, 1:2], op0=ALU.mult, op1=ALU.add))
            DVE.wait(d)
            d = DVE.done(nc.vector.tensor_scalar(out=ms2, in0=ms, scalar1=EPS, scalar2=None, op0=ALU.add))
            POOL.wait(d)
            p = POOL.done(nc.gpsimd.tensor_tensor(out=rstd, in0=ms2, in1=MHALF, op=ALU.pow))
            return p, rstd

        u32alt_free = [None]

        def norm_transpose_g(x_ap, x_tok, ui, dst_ap, dst_free, owner=None, raw=False):
            p, rstd = rstd_from_stats(x_ap, x_tok)
            if raw:
                UB = x_ap
                p2 = x_tok
                ufree = [None]
                yield
            else:
                yield from delay(5)
                if p1_mode[0] and owner == "tail":
                    UB = QT[:, 0, :, :].rearrange("p a n -> p (a n)")[:, 0:1024]
                    ufree = u32alt_free
                else:
                    UB = U32[:].bitcast(BF16)[:, 0:1024]
                    ufree = u32_free
                DVE.wait(p, ufree[0])
                p2 = DVE.done(nc.vector.tensor_scalar(out=UB, in0=x_ap, scalar1=rstd, scalar2=None, op0=ALU.mult))
                yield from delay(4)
            d2 = None
            if raw:
                for h in range(2):
                    b = fbank(owner)
                    PE.wait(p2, bank_free[b])
                    for k in range(4):
                        kk = h * 4 + k
                        tr = nc.tensor.transpose(out=bank(b)[:, k * 128:(k + 1) * 128], in_=UB[:, kk * 128:(kk + 1) * 128], identity=ident32)
                    tp = PE.done(tr)
                    yield from delay(3)
                    DVE.wait(tp, dst_free)
                    d2 = DVE.done(nc.vector.tensor_copy(out=dst_ap[:, h * 4:(h + 1) * 4, :], in_=bank(b).rearrange("p (k n) -> p k n", k=4)))
                    bank_free[b] = d2
            else:
                b = fbank(owner)
                PE.wait(p2, bank_free[b])
                pb = bankb(b)
                for k in range(8):
                    tr = nc.tensor.transpose(out=pb[:, k * 128:(k + 1) * 128], in_=UB[:, k * 128:(k + 1) * 128], identity=identb)
                    if k == 3:
                        yield
                tp = PE.done(tr)
                yield from delay(3)
                DVE.wait(tp, dst_free)
                d2 = DVE.done(nc.vector.tensor_copy(out=dst_ap, in_=pb.rearrange("p (k n) -> p k n", k=8)))
                bank_free[b] = d2
            ufree[0] = tp
            if raw:
                return d2, tp, p, rstd
            return d2, p2

        ydone_keys = ["yo0", "yo1"]

        def phase1(slot, filler=None):
            var = 0 if slot < 2 else 1
            xfree = [None, None]
            utb = [UTf[:, :, 0:128], UTf[:, :, 128:256]]
            ut_free = [None, None]
            rk = [RQ[:, 0:256], RQ[:, 256:512]]
            rk_free = [None, None]
            d1_free = [None, None]
            Zb = PT[:, 0, :]
            Yb = PT[:, 1, :]
            pst = dict(z_free=None, y_free=None, t12_free=None)
            resA = {}

            xtok = {}
            for t0 in range(2):
                xtok[t0] = dma(XS[:, t0, :], xs[slot, t0], f"xs{t0}", waits=[xfree[t0]])

            def stageA(t):
                i = t % 2
                tx = xtok.pop(t)
                trk = dma(rk[i], ropek[t], f"rk{i}", waits=[rk_free[i]])
                td1 = dma(D1[:, i, :], dft1[var, t], f"d1{i}", waits=[d1_free[i]])
                u_tok, xrd = yield from norm_transpose_g(XS[:, i, :], tx, 0, utb[i], ut_free[i], owner="A")
                xfree[i] = xrd
                if t + 2 < NT:
                    xtok[t + 2] = dma(XS[:, i, :], xs[slot, t + 2], f"xs{i}", waits=[xrd])
                resA[t] = (u_tok, trk, td1)

            def stageB(t):
                i = t % 2
                u_tok, trk, td1 = resA.pop(t)
                uT = utb[i]
                PE.wait(u_tok, bank_free[0], bank_free[1], bank_free[2])
                for k in range(8):
                    nc.tensor.matmul(bank(0), lhsT=uT[:, k, :], rhs=W1F[:, k, 0:512], start=(k == 0), stop=(k == 7))
                    nc.tensor.matmul(bank(1), lhsT=uT[:, k, :], rhs=W1F[:, k, 512:1024], start=(k == 0), stop=(k == 7))
                    mz = nc.tensor.matmul(bank(2)[:, 0:128], lhsT=uT[:, k, :], rhs=W1[:, k, 256:384], start=(k == 0), stop=(k == 7))
                    if k % 2 == 1 and k < 7:
                        yield
                mz = PE.done(mz)
                yield
                PE.wait(bank_free[3])
                for k in range(8):
                    nc.tensor.matmul(bank(3)[:, 0:128], lhsT=W1[:, k, 0:128], rhs=uT[:, k, :], start=(k == 0), stop=(k == 7))
                yield
                for k in range(8):
                    mk = nc.tensor.matmul(bank(3)[:, 128:256], lhsT=W1[:, k, 128:256], rhs=uT[:, k, :], start=(k == 0), stop=(k == 7))
                mk = PE.done(mk)
                ut_free[i] = mk
                ACT.wait(mz, pst["z_free"])
                az = ACT.done(nc.scalar.copy(out=Zb, in_=PS[0][:, :]))
                bank_free[0] = az
                bank_free[1] = az
                DVE.wait(mz)
                nc.vector.tensor_copy(out=VA[:, t, 0, 0:64], in_=bank(2)[:, 0:64])
                dv = DVE.done(nc.vector.tensor_copy(out=VA[:, t, 1, 64:128], in_=bank(2)[:, 64:128]))
                yield
                bq, bqr = bank(3)[:, 0:128], bank(3)[:, 128:256]
                ACT.wait(mk, PE.last())
                a1 = ACT.done(nc.scalar.activation(out=SQ[:, 0:128], in_=bq, func=AF.Square))
                PE.wait(a1, dv)
                m = PE.done(nc.tensor.matmul(bank(2)[:, 0:128], lhsT=bdones, rhs=SQ[:, 0:128], start=True, stop=True))
                PE.wait(az, td1, bank_free[4], bank_free[5])
                nc.tensor.matmul(bank(4), lhsT=D1[:, i, 0:128], rhs=Zb[:, 0:512], start=True, stop=False)
                nc.tensor.matmul(bank(4), lhsT=D1[:, i, 128:256], rhs=Zb[:, 512:1024], start=False, stop=True)
                nc.tensor.matmul(bank(5), lhsT=D1[:, i, 0:128], rhs=Zb[:, 512:1024], start=True, stop=False)
                my = PE.done(nc.tensor.matmul(bank(5), lhsT=D1[:, i, 256:384], rhs=Zb[:, 0:512], start=False, stop=True))
                pst["z_free"] = my
                d1_free[i] = my
                yield
                ACT.wait(m, pst["t12_free"])
                a2 = ACT.done(nc.scalar.activation(out=RS[:, 0:128], in_=bank(2)[:, 0:128], func=AF.Sqrt, scale=1.0 / 64, bias=EPSA))
                bank_free[2] = a2
                DVE.wait(mk, trk, pst["t12_free"], a1)
                d1 = DVE.done(nc.vector.scalar_tensor_tensor(out=T1[:, 0:128], in0=bq, scalar=GV[:, 18:19], in1=rk[i][:, 0:128], op0=ALU.mult, op1=ALU.mult))
                d2 = DVE.done(nc.vector.scalar_tensor_tensor(out=T2[:, 0:128], in0=bqr, scalar=GV[:, 19:20], in1=rk[i][:, 128:256], op0=ALU.mult, op1=ALU.mult))
                bank_free[3] = [a1, d2]
                rk_free[i] = d2
                DVE.wait(a2)
                a3 = DVE.done(nc.vector.reciprocal(out=RS[:, 0:128], in_=RS[:, 0:128]))
                DVE.wait(my, pst["y_free"])
                dy = DVE.done(nc.vector.tensor_copy(out=Yb, in_=PS[2][:, :]))
                bank_free[4] = dy
                bank_free[5] = dy
                yield
                POOL.wait(d1, d2)
                p1 = POOL.done(nc.gpsimd.tensor_tensor(out=T1[:, 0:128], in0=T1[:, 0:128], in1=T2[:, 0:128], op=ALU.add))
                POOL.wait(p1, a3)
                p2 = POOL.done(nc.gpsimd.tensor_tensor(out=KT[:, t * 128:(t + 1) * 128], in0=T1[:, 0:128], in1=RS[:, 0:128], op=ALU.mult))
                pst["t12_free"] = p2
                ydst = ysc[slot].rearrange("r g p c -> g r p c")
                for b_lo in range(2):
                    for alo in range(2):
                        p0 = b_lo * 64 + alo * 32
                        ty = dma(ydst[:, :, alo * 64 + 2 * t + b_lo, :], Yb[p0:p0 + 32, :].rearrange("p (r c) -> p r c", r=2),
                                 "ysc", waits=[dy])
                pst["y_free"] = ty
                yield

            p1_mode[0] = filler is not None
            run(stageA(0))
            for t in range(NT):
                gb = stageB(t)
                ga = stageA(t + 1) if t + 1 < NT else None
                while gb is not None or ga is not None:
                    if gb is not None:
                        try:
                            next(gb)
                        except StopIteration:
                            gb = None
                    if ga is not None:
                        try:
                            next(ga)
                        except StopIteration:
                            ga = None
                    if filler is not None:
                        try:
                            next(filler)
                        except StopIteration:
                            filler = None
            run(filler)
            p1_mode[0] = False

        st = dict(rq_free=None, xs_free=[None, None], utf_free=None, sq_free=None, rs_free=None, t1_free=None, t2_free=None,
                  eg_free=None, ys_free=None, oa_free=None, ht_free=[None, None], ptl_free=None,
                  ptt_free=None, et_free=None, t2t_free=None, u2t_free=None)
        qt_ready = [[None] * 4, [None] * 4]
        qt_free = [None, None]
        gm_ready = [[None] * 8, [None] * 8]
        gm_free = [None, None]
        am_ready = [[None] * 4, [None] * 4]

        def proj_g(cc, u_tok):
            b = fbank()
            PE.wait(u_tok, bank_free[b])
            for k in range(8):
                mm = nc.tensor.matmul(bank(b), lhsT=W2[:, k, cc * 128:(cc + 1) * 128], rhs=UTf[:, k, :], start=(k == 0), stop=(k == 7))
                if k == 3:
                    yield
            return b, PE.done(mm)

        def front_prefetch(qt):
            trq = dma(RQ[:], ropeq[qt], "rq", waits=[st["rq_free"]])
            txs = {}
            for sub in range(2):
                txs[sub] = dma(XS[:, sub, :], xq[qt, sub * 128:(sub + 1) * 128, :], f"xs{sub}", waits=[st["xs_free"][sub]])
            return trq, txs

        def front_g(slot, j, qt, qi, pre=None):
            qt_ready[qi] = [None] * 4
            gm_ready[qi] = [None] * 8
            if pre is None:
                pre = front_prefetch(qt)
                yield from delay(3)
            trq, txs = pre
            u_tok = None
            for sub in range(4):
                i = sub % 2
                u_tok, xrd = yield from norm_transpose_g(XS[:, i, :], txs[sub], 0, UTf[:, :, sub * 128:(sub + 1) * 128],
                                                         st["utf_free"] if sub == 0 else None)
                st["xs_free"][i] = xrd
                if sub + 2 < 4:
                    txs[sub + 2] = dma(XS[:, i, :], xq[qt, (sub + 2) * 128:(sub + 3) * 128, :], f"xs{i}", waits=[xrd])
                yield
            yield from delay(2)
            for p in range(4):
                bq, mq = yield from proj_g(p, u_tok)
                yield from delay(2)
                ACT.wait(mq, st["sq_free"])
                a1 = ACT.done(nc.scalar.activation(out=SQ[:], in_=bank(bq), func=AF.Square))
                bs = fbank()
                PE.wait(a1, bank_free[bs])
                m = PE.done(nc.tensor.matmul(bank(bs), lhsT=bdones, rhs=SQ[:], start=True, stop=True))
                st["sq_free"] = m
                DVE.wait(mq, trq, st["t1_free"], a1)
                d1 = DVE.done(nc.vector.scalar_tensor_tensor(out=T1[:], in0=bank(bq), scalar=GV[:, 16:17], in1=RQ[:, 0:512], op0=ALU.mult, op1=ALU.mult))
                bank_free[bq] = [a1, d1]
                yield from delay(2)
                ACT.wait(m, st["rs_free"])
                a2 = ACT.done(nc.scalar.activation(out=RS[:], in_=bank(bs), func=AF.Sqrt, scale=1.0 / 64, bias=EPSA))
                bank_free[bs] = a2
                DVE.wait(a2)
                a3 = DVE.done(nc.vector.reciprocal(out=RS[:], in_=RS[:]))
                br, mr = yield from proj_g(4 + p, u_tok)
                yield from delay(2)
                DVE.wait(mr, st["t2_free"])
                d2 = DVE.done(nc.vector.scalar_tensor_tensor(out=T2[:], in0=bank(br), scalar=GV[:, 17:18], in1=RQ[:, 512:1024], op0=ALU.mult, op1=ALU.mult))
                bank_free[br] = d2
                POOL.wait(d1, d2)
                p1 = POOL.done(nc.gpsimd.tensor_tensor(out=T1[:], in0=T1[:], in1=T2[:], op=ALU.add))
                st["t2_free"] = p1
                POOL.wait(p1, a3, qt_free[qi])
                p2 = POOL.done(nc.gpsimd.tensor_tensor(out=QT[:, qi, p, :], in0=T1[:], in1=RS[:], op=ALU.mult))
                st["t1_free"] = p2
                st["rs_free"] = p2
                qt_ready[qi][p] = p2
                st["rq_free"] = [d1, d2]
                yield
            for c in range(8):
                bg, mg = yield from proj_g(8 + c, u_tok)
                if c == 7:
                    st["utf_free"] = mg
                    for r in range(2):
                        for al in range(4):
                            tys = dma(YS[:, r * 4 + al, :], ysc[slot, r, 4 * j + al, :, :], "ysb", waits=[st["ys_free"], st["utf_free"]])
                yield from delay(2)
                ACT.wait(mg, st["eg_free"])
                a1 = ACT.done(nc.scalar.activation(out=EG[:], in_=bank(bg), func=AF.Tanh, scale=0.5))
                yield
                DVE.wait(a1, gm_free[qi])
                pg = DVE.done(nc.vector.scalar_tensor_tensor(out=GM[:, qi, c, :], in0=EG[:], scalar=1.0, in1=bank(bg), op0=ALU.add, op1=ALU.mult))
                bank_free[bg] = pg
                st["eg_free"] = pg
                gm_ready[qi][c] = pg
            yield from delay(4)
            for c in range(4):
                b = fbank()
                PE.wait(tys, bank_free[b])
                for al in range(4):
                    nc.tensor.matmul(bank(b)[:, al * 128:(al + 1) * 128], lhsT=YS[:, al, c * 128:(c + 1) * 128], rhs=D3[:, 0:128], start=True, stop=False)
                    mf = nc.tensor.matmul(bank(b)[:, al * 128:(al + 1) * 128], lhsT=YS[:, 4 + al, c * 128:(c + 1) * 128], rhs=D3[:, 128:256], start=False, stop=True)
                    if al == 1:
                        yield
                mf = PE.done(mf)
                yield from delay(2)
                DVE.wait(mf, gm_ready[qi][4 + c])
                df = DVE.done(nc.vector.scalar_tensor_tensor(out=GM[:, qi, 4 + c, :], in0=bank(b), scalar=0.5, in1=GM[:, qi, 4 + c, :], op0=ALU.mult, op1=ALU.mult))
                bank_free[b] = df
                gm_ready[qi][4 + c] = df
            st["ys_free"] = mf
            st["utf_free"] = mf

        def tail_g(slot, j, qt, qi):
            txs = {}
            tps = {}
            for sub in range(2):
                txs[sub] = dma(HT[:, sub, :], xq[qt, sub * 128:(sub + 1) * 128, :], f"ht{sub}", waits=[st["ht_free"][sub]])
            tps[0] = dma(PTL[:], pq[qt, 0:128, :], "ptl", waits=[st["ptl_free"]])
            yield from delay(2)
            for sub in range(4):
                i = sub % 2
                tsl = slice(sub * 128, (sub + 1) * 128)
                tx = txs[sub]
                tp_ = tps[sub]
                for half in range(2):
                    hs = slice(half * 512, (half + 1) * 512)
                    b = fbank("tail")
                    PE.wait(am_ready[qi], gm_ready[qi][4:8], bank_free[b])
                    for kc in range(8):
                        mm = nc.tensor.matmul(bank(b), lhsT=GM[:, qi, kc, tsl], rhs=WO[:, kc, hs], start=(kc == 0), stop=(kc == 7))
                        if kc == 3:
                            yield
                    mm = PE.done(mm)
                    yield from delay(2)
                    DVE.wait(mm, tx)
                    dh = DVE.done(nc.vector.tensor_tensor(out=HT[:, i, hs], in0=bank(b), in1=HT[:, i, hs], op=ALU.add))
                    bank_free[b] = dh
                if sub == 3:
                    gm_free[qi] = mm
                u2_tok, tr_tok, pr2, rstd2 = yield from norm_transpose_g(HT[:, i, :], dh, 1, U2T[:, :, :], st["u2t_free"], owner="tail", raw=True)
                hsc = smcols(1)
                DVE.wait(pr2)
                dsc = DVE.done(nc.vector.tensor_scalar(out=hsc, in0=rstd2, scalar1=0.5, scalar2=None, op0=ALU.mult))
                b = fbank("tail")
                PE.wait(tp_, bank_free[b])
                nc.tensor.transpose(out=bank(b)[:, 0:128], in_=PTL[:, 0:128], identity=ident32)
                mp = PE.done(nc.tensor.transpose(out=bank(b)[:, 128:256], in_=PTL[:, 128:256], identity=ident32))
                st["ptl_free"] = mp
                if sub + 1 < 4:
                    tps[sub + 1] = dma(PTL[:], pq[qt, (sub + 1) * 128:(sub + 2) * 128, :], "ptl", waits=[mp])
                yield from delay(2)
                DVE.wait(mp, st["ptt_free"])
                dp = DVE.done(nc.vector.tensor_copy(out=PTT[:, :, :], in_=bank(b)[:, 0:256].rearrange("p (k n) -> p k n", k=2)))
                bank_free[b] = dp
                for half in range(2):
                    hs = slice(half * 512, (half + 1) * 512)
                    bg = fbank("tail")
                    PE.wait(u2_tok, bank_free[bg])
                    for k in range(8):
                        mg = nc.tensor.matmul(bank(bg), lhsT=U2T[:, k, :], rhs=WPG[:, k, hs], start=(k == 0), stop=(k == 7))
                        if k == 3:
                            yield
                    mg = PE.done(mg)
                    yield from delay(2)
                    ACT.wait(mg, st["et_free"], dsc)
                    a1 = ACT.done(nc.scalar.activation(out=ET[:], in_=bank(bg), func=AF.Tanh, scale=hsc))
                    bank_free[bg] = a1
                    bp = fbank("tail")
                    PE.wait(dp, bank_free[bp])
                    nc.tensor.matmul(bank(bp), lhsT=PTT[:, 0, :], rhs=WPLE[:, 0, hs], start=True, stop=False)
                    mpl = PE.done(nc.tensor.matmul(bank(bp), lhsT=PTT[:, 1, :], rhs=WPLE[:, 1, hs], start=False, stop=True))
                    yield from delay(2)
                    DVE.wait(a1, mpl, st["t2t_free"])
                    dt_ = DVE.done(nc.vector.scalar_tensor_tensor(out=T2t[:], in0=ET[:], scalar=1.0, in1=bank(bp), op0=ALU.add, op1=ALU.mult))
                    bank_free[bp] = dt_
                    st["et_free"] = dt_
                    POOL.wait(dt_, tr_tok)
                    ph = POOL.done(nc.gpsimd.tensor_tensor(out=HT[:, i, hs], in0=HT[:, i, hs], in1=T2t[:], op=ALU.add))
                    st["t2t_free"] = ph
                st["u2t_free"] = mg
                st["ptt_free"] = mpl
                pr, rstd = rstd_from_stats(HT[:, i, :], ph)
                yield
                DVE.wait(pr)
                dyo = DVE.done(nc.vector.scalar_tensor_tensor(out=HT[:, i, :], in0=HT[:, i, :], scalar=rstd, in1=GFIN[:], op0=ALU.mult, op1=ALU.mult))
                st["ht_free"][i] = dma(y[qt, tsl, :], HT[:, i, :], ydone_keys[i], waits=[dyo])
                if sub + 2 < 4:
                    txs[sub + 2] = dma(HT[:, i, :], xq[qt, (sub + 2) * 128:(sub + 3) * 128, :], f"ht{i}", waits=[st["ht_free"][i]])
                yield

        pt_free = [None, None, None]

        def attention(qi, filler, rate_num=5, rate_den=2):
            n_it = 4 * NT
            ms = {}

            fstate = [filler]

            def need(tokref):
                while tokref() is None:
                    next(fstate[0])

            def qk(it):
                p, kt = divmod(it, NT)
                si = it % 2
                need(lambda: qt_ready[qi][p])
                PE.wait(qt_ready[qi][p], bank_free[2 * si], bank_free[2 * si + 1])
                nc.tensor.matmul(bank(2 * si), lhsT=KT[0:64, kt * 128:(kt + 1) * 128], rhs=QT[0:64, qi, p, :], start=True, stop=True)
                return PE.done(nc.tensor.matmul(bank(2 * si + 1), lhsT=KT[64:128, kt * 128:(kt + 1) * 128], rhs=QT[64:128, qi, p, :], start=True, stop=True))

            ms[0] = qk(0)
            ms[1] = qk(1)
            acc = 0
            nitems = [0]
            fin_it = [None]
            for it in range(n_it):
                p, kt = divmod(it, NT)
                si = it % 2
                pi = it % 3
                ACT.wait(ms.pop(it), pt_free[pi])
                ae = ACT.done(nc.scalar.activation(out=PT[:, pi, :], in_=PS[si][:, :], func=AF.Exp, scale=0.125))
                bank_free[2 * si] = ae
                bank_free[2 * si + 1] = ae
                if it + 2 < n_it:
                    ms[it + 2] = qk(it + 2)
                else:
                    qt_free[qi] = PE.last()
                if kt == 0:
                    PE.wait(bank_free[4], bank_free[5])
                PE.wait(ae)
                nc.tensor.matmul(bank(4), lhsT=VA[:, kt, 0, :], rhs=PT[:, pi, 0:512], start=(kt == 0), stop=(kt == NT - 1))
                mo = PE.done(nc.tensor.matmul(bank(5), lhsT=VA[:, kt, 1, :], rhs=PT[:, pi, 512:1024], start=(kt == 0), stop=(kt == NT - 1)))
                pt_free[pi] = mo
                if kt == NT - 1:
                    DVE.wait(mo, st["oa_free"])
                    nc.vector.tensor_copy(out=RSa[0:64, :], in_=bank(4)[64:128, :])
                    nc.vector.tensor_copy(out=RSa[64:128, :], in_=bank(5)[0:64, :])
                    nc.vector.tensor_copy(out=OAa[0:64, :], in_=bank(4)[0:64, :])
                    dc = DVE.done(nc.vector.tensor_copy(out=OAa[64:128, :], in_=bank(5)[64:128, :]))
                    bank_free[4] = dc
                    bank_free[5] = dc
                    DVE.wait(dc)
                    dr = DVE.done(nc.vector.reciprocal(out=RSa[:], in_=RSa[:]))
                    DVE.wait(dr)
                    dn = DVE.done(nc.vector.scalar_tensor_tensor(out=OAa[:], in0=OAa[:], scalar=0.5, in1=RSa[:], op0=ALU.mult, op1=ALU.mult))
                    need(lambda: gm_ready[qi][p])
                    POOL.wait(dn, gm_ready[qi][p])
                    pg = POOL.done(nc.gpsimd.tensor_tensor(out=GM[:, qi, p, :], in0=OAa[:], in1=GM[:, qi, p, :], op=ALU.mult))
                    st["oa_free"] = pg
                    am_ready[qi][p] = pg
                if fstate[0] is not None:
                    acc += rate_num
                    while acc >= rate_den:
                        acc -= rate_den
                        try:
                            next(fstate[0])
                            nitems[0] += 1
                        except StopIteration:
                            fstate[0] = None
                            fin_it[0] = it
                            break
            run(fstate[0])

        def chain(*gens):
            for g in gens:
                if g is not None:
                    yield from g

        qt_base = 0
        gq = 0
        pending_tail = None
        for slot in range(n_slots):
            if slot == 0:
                barrier()
            else:
                barrier(with_dma_keys=["ysb"])
            phase1(slot, pending_tail if slot > 0 else prep_rest_g())
            pending_tail = None
            barrier(with_dma_keys=["ysc", "st1"])
            nq = slot_qt[slot]
            if nq == 0:
                qt_base += SLOT_QT[slot]
                continue
            g0 = front_g(slot, 0, qt_base, gq % 2)
            next(g0)
            while qt_ready[gq % 2][0] is None:
                next(g0)
            for j in range(nq):
                qi = (gq + j) % 2
                fl = []
                if j == 0:
                    fl.append(g0)
                if j > 0:
                    fl.append(tail_g(slot, j - 1, qt_base + j - 1, (gq + j - 1) % 2))
                if j + 1 < nq:
                    pre = front_prefetch(qt_base + j + 1) if j > 0 else None
                    fl.append(front_g(slot, j + 1, qt_base + j + 1, (gq + j + 1) % 2, pre))
                attention(qi, chain(*fl) if fl else None)
            pending_tail = tail_g(slot, nq - 1, qt_base + nq - 1, (gq + nq - 1) % 2)
            if slot == n_slots - 1:
                run(pending_tail)
            gq += nq
            qt_base += SLOT_QT[slot]
        for k in ydone_keys:
            if k in dsems:
                SP.wait(dtok(k))
                for e in COMPUTE:
                    e.wait(dtok(k))
    return nc


def _rope_tables():
    f32 = np.float32
    inv = (f32(10000.0) ** (-(np.arange(16, dtype=f32)) / f32(16))).astype(f32)
    s = np.arange(S)
    row = (s // 64).astype(f32)
    col = (s % 64).astype(f32)
    ar = row[:, None] * inv[None, :]
    ac = col[:, None] * inv[None, :]
    ang = np.concatenate([ar, ar, ac, ac], axis=-1).astype(f32)
    cos = np.cos(ang).astype(f32)
    sin = np.sin(ang).astype(f32)
    sign = np.ones(64, f32)
    sign[0:16] = -1
    sign[32:48] = -1
    return cos, sin * sign[None, :]


PERM64 = np.concatenate([np.arange(16, 32), np.arange(0, 16), np.arange(48, 64), np.arange(32, 48)])


def _p1_tokens():
    t = np.arange(NT)[:, None]
    p = np.arange(128)[None, :]
    return 64 * (p % 64) + 2 * t + (p // 64)


def _qt_tokens(j):
    n = np.arange(512)
    al, alo, bp = n // 128, (n // 64) % 2, n % 64
    return (8 * j + 2 * al + alo) + 64 * bp


def _dft_tables(half):
    out = np.zeros((NT, 128, 3, 128), np.float64)
    a = np.arange(64)
    mm = np.arange(64)
    alo, g = mm // 32, mm % 32
    ap = (2 * g + alo + 32 * half) % 64
    for t in range(NT):
        for b_lo in range(2):
            b = 2 * t + b_lo
            th = 2 * np.pi * (np.outer(a, ap) / 64.0 + (b * ap)[None, :] / 4096.0)
            sl = slice(b_lo * 64, b_lo * 64 + 64)
            out[t, sl, 0, sl] = np.cos(th)
            out[t, sl, 1, sl] = np.sin(th)
            out[t, sl, 2, sl] = -np.sin(th)
    return out.reshape(NT, 128, 384)


_CONST_CACHE = {}


def _consts():
    if _CONST_CACHE:
        return _CONST_CACHE
    cos, sins = _rope_tables()
    c = _CONST_CACHE
    c["cos"], c["sins"] = cos, sins
    bd = np.zeros((128, 128), np.float32)
    bd[0:64, 0:64] = 1
    bd[64:128, 64:128] = 1
    cc = np.arange(64)
    th = 2 * np.pi * np.outer(cc, cc) / 64.0
    cst = np.zeros((128, 384), np.float32)
    cst[:, 0:128] = bd
    cst[:, 128:256] = np.eye(128, dtype=np.float32)
    cst[0:64, 256:320] = (np.cos(th) / 512.0).astype(np.float32)
    cst[0:64, 320:384] = (-np.sin(th) / 512.0).astype(np.float32)
    c["cst32"] = cst
    d3 = np.zeros((128, 384), np.float64)
    d3[:, 256:384] = np.eye(128)
    for alo in range(2):
        sl = slice(alo * 64, alo * 64 + 64)
        d3[sl, 0:128][:, sl] = np.cos(th)
        d3[sl, 128:256][:, sl] = np.sin(th)
    c["dft3"] = d3.astype(ml_dtypes.bfloat16)
    c["dft1"] = [_dft_tables(0).astype(ml_dtypes.bfloat16), _dft_tables(1).astype(ml_dtypes.bfloat16)]
    tok1 = _p1_tokens()
    rk = np.zeros((NT, 128, 256), np.float32)
    for t in range(NT):
        ct = cos[tok1[t]].T
        st = sins[tok1[t]].T
        rk[t, 0:64, 0:128] = ct
        rk[t, 64:128, 0:128] = ct
        rk[t, 0:64, 128:256] = st
        rk[t, 64:128, 128:256] = st
    c["ropek"] = rk
    rq = np.zeros((8, 128, 1024), np.float32)
    for j in range(8):
        tk = _qt_tokens(j)
        ct = cos[tk].T
        st = sins[tk].T
        rq[j, 0:64, 0:512] = ct
        rq[j, 64:128, 0:512] = ct
        rq[j, 0:64, 512:1024] = st
        rq[j, 64:128, 512:1024] = st
    c["ropeq8"] = rq
    c["tok1"] = tok1
    c["qtok"] = np.stack([_qt_tokens(j) for j in range(8)])
    return c


def _weights(g_norm, w_in, g_q, g_k, w_fmix, w_out, g_ple, w_ple_gate, w_ple, g_final):
    w_in = np.asarray(w_in[0], np.float32)
    q, k, v = w_in[:, 0:512], w_in[:, 512:640], w_in[:, 640:768]
    ga, f, gf = w_in[:, 768:1280], w_in[:, 1280:1792], w_in[:, 1792:2304]
    pair_cols = np.concatenate([np.concatenate([np.arange(p * 64, p * 64 + 64), np.arange((p + 4) * 64, (p + 4) * 64 + 64)]) for p in range(4)])
    rot_cols512 = (np.arange(512) // 64) * 64 + PERM64[np.arange(512) % 64]
    rot_cols128 = (np.arange(128) // 64) * 64 + PERM64[np.arange(128) % 64]
    qrot = q[:, rot_cols512]
    krot = k[:, rot_cols128]
    w1 = np.concatenate([k, krot, v], axis=1).reshape(8, 128, 384)
    w2 = np.concatenate([q[:, pair_cols], qrot[:, pair_cols], ga[:, pair_cols], gf], axis=1).reshape(8, 128, 2048)
    wfT = np.ascontiguousarray(f.T.reshape(8, 64, D).transpose(1, 0, 2))
    wo_rows = np.concatenate([pair_cols, np.arange(512, 1024)])
    wo = np.asarray(w_out[0], np.float32)[wo_rows].reshape(8, 128, D)
    wpg = np.asarray(w_ple_gate[0], np.float32).reshape(8, 128, D)
    wple = np.asarray(w_ple[0], np.float32).reshape(2, 128, D)
    wfm = np.ascontiguousarray(np.asarray(w_fmix[0], np.float32).transpose(1, 0, 2))
    gv = np.zeros((128, 24), np.float32)
    gv[:, 0:8] = np.asarray(g_norm[0], np.float32).reshape(8, 128).T
    gv[:, 8:16] = np.asarray(g_ple[0], np.float32).reshape(8, 128).T
    gq = np.asarray(g_q[0], np.float32)
    gk = np.asarray(g_k[0], np.float32)
    gv[:, 16] = np.tile(gq, 2)
    gv[:, 17] = np.tile(gq[PERM64], 2)
    gv[:, 18] = np.tile(gk, 2)
    gv[:, 19] = np.tile(gk[PERM64], 2)
    gfin = np.ascontiguousarray(np.broadcast_to(np.asarray(g_final, np.float32)[None, :], (128, D)))
    return dict(w1=np.ascontiguousarray(w1), w2=np.ascontiguousarray(w2), wfT=wfT, wo=np.ascontiguousarray(wo),
                wpg=np.ascontiguousarray(wpg), wple=np.ascontiguousarray(wple), wfm=wfm, gvec=gv, gfin=gfin)


def _core_plan(c):
    return [(2 * c, None), (2 * c + 1, None), (16 + c // 2, c % 2)]


def _prepare(x_prompt, x_sample, p_prompt, p_sample, g_norm, w_in, g_q, g_k, w_fmix, w_out,
             g_ple, w_ple_gate, w_ple, g_final, cores=range(NCORES)):
    cst = _consts()
    x_prompt = np.asarray(x_prompt, np.float32)
    x_sample = np.asarray(x_sample, np.float32)
    p_prompt = np.asarray(p_prompt, np.float32)[0]
    p_sample = np.asarray(p_sample, np.float32)[0]
    nb_p = x_prompt.shape[0]

    def xseq(i):
        return x_prompt[i] if i < nb_p else x_sample[i - nb_p]

    def pseq(i):
        return p_prompt[i] if i < nb_p else p_sample[i - nb_p]

    wts = _weights(g_norm, w_in, g_q, g_k, w_fmix, w_out, g_ple, w_ple_gate, w_ple, g_final)
    tok1 = cst["tok1"].reshape(-1)
    in_maps = []
    plans = []
    for c in cores:
        plan = _core_plan(c)
        xs = np.empty((3, NT, 128, D), np.float32)
        xq = np.empty((NQT, 512, D), np.float32)
        pq = np.empty((NQT, 512, 256), np.float32)
        rq = np.empty((NQT, 128, 1024), np.float32)
        qmap = []
        qi = 0
        for slot, (si, half) in enumerate(plan):
            xx = xseq(si)
            pp = pseq(si)
            xs[slot] = xx[tok1].reshape(NT, 128, D)
            js = range(8) if half is None else range(4 * half, 4 * half + 4)
            for j in js:
                tk = cst["qtok"][j]
                xq[qi] = xx[tk]
                pq[qi] = pp[tk]
                rq[qi] = cst["ropeq8"][j]
                qmap.append((si, j))
                qi += 1
        d1 = np.stack([cst["dft1"][0], cst["dft1"][plan[2][1]]])
        m = dict(xs=xs, xq=xq, pq=pq, ropeq=rq, dft1=d1, dft3=cst["dft3"], cst32=cst["cst32"], ropek=cst["ropek"])
        m.update(wts)
        in_maps.append(m)
        plans.append(qmap)
    return in_maps, plans, nb_p


def kernel(x_prompt, x_sample, p_prompt, p_sample, g_norm, w_in, g_q, g_k, w_fmix, w_out,
           g_ple, w_ple_gate, w_ple, g_final):
    cst = _consts()
    in_maps, plans, nb_p = _prepare(x_prompt, x_sample, p_prompt, p_sample, g_norm, w_in, g_q, g_k, w_fmix, w_out,
                                    g_ple, w_ple_gate, w_ple, g_final)
    nc = build_nc()
    res = run_bass_kernel_spmd(nc, in_maps, core_ids=list(range(NCORES)))
    y_all = np.empty((20, S, D), np.float32)
    for c in range(NCORES):
        yc = res.results[c]["y"]
        for qi, (si, j) in enumerate(plans[c]):
            y_all[si][cst["qtok"][j]] = yc[qi]
    return (np.ascontiguousarray(y_all[:nb_p]), np.ascontiguousarray(y_all[nb_p:]))
```

```python
import numpy as np
import ml_dtypes
from contextlib import ExitStack
import concourse.bass as bass
import concourse.mybir as mybir
from concourse.bass_utils import run_bass_kernel_spmd

F32 = mybir.dt.float32
BF16 = mybir.dt.bfloat16
AF = mybir.ActivationFunctionType
ALU = mybir.AluOpType

NCORES = 8
S = 4096
D = 1024
NT = 32
NQT = 20
EPS = 1e-6
SLOT_QT = [8, 8, 4]


class Eng:
    def __init__(self, e, sem, name):
        self.e = e
        self.sem = sem
        self.name = name
        self.count = 0
        self.waited = {}

    def wait(self, *toks):
        for tok in toks:
            if tok is None:
                continue
            if isinstance(tok, list):
                self.wait(*tok)
                continue
            key, sem, cnt = tok
            if self.waited.get(key, 0) >= cnt:
                continue
            self.e.wait_ge(sem, cnt)
            self.waited[key] = cnt

    def done(self, inst):
        self.count += 1
        inst.then_inc(self.sem, 1)
        return (self.name, self.sem, self.count)

    def last(self):
        if self.count == 0:
            return None
        return (self.name, self.sem, self.count)


def build_nc(n_slots=3, slot_qt=SLOT_QT, overlap=True):
    nc = bass.Bass("TRN2", target_bir_lowering=False)
    dt_in = lambda n, s, d=F32: nc.dram_tensor(n, s, d, kind="ExternalInput").ap()
    xs = dt_in("xs", [3, NT, 128, D])
    xq = dt_in("xq", [NQT, 512, D])
    pq = dt_in("pq", [NQT, 512, 256])
    w1 = dt_in("w1", [8, 128, 384])
    wfT = dt_in("wfT", [64, 8, D])
    w2 = dt_in("w2", [8, 128, 2048])
    wo = dt_in("wo", [8, 128, D])
    wpg = dt_in("wpg", [8, 128, D])
    wple = dt_in("wple", [2, 128, D])
    wfm = dt_in("wfm", [64, 8, 64])
    gvec = dt_in("gvec", [128, 24])
    gfin = dt_in("gfin", [128, D])
    cst32 = dt_in("cst32", [128, 384])
    dft3 = dt_in("dft3", [128, 384], BF16)
    dft1 = dt_in("dft1", [2, NT, 128, 384], BF16)
    ropek = dt_in("ropek", [NT, 128, 256])
    ropeq = dt_in("ropeq", [NQT, 128, 1024])
    y = nc.dram_tensor("y", [NQT, 512, D], F32, kind="ExternalOutput").ap()
    ysc = nc.dram_tensor("ysc", [3, 2, 32, 128, 512], BF16, kind="ExternalOutput").ap()

    with ExitStack() as es:
        sb = lambda n, s, d: es.enter_context(nc.sbuf_tensor(n, s, d))
        W1 = sb("W1", [128, 8, 384], BF16)
        W1F = sb("W1F", [128, 8, 1024], BF16)
        W2 = sb("W2", [128, 8, 2048], BF16)
        WO = sb("WO", [128, 8, 1024], BF16)
        WPG = sb("WPG", [128, 8, 1024], BF16)
        WPLE = sb("WPLE", [128, 2, 1024], BF16)
        KT = sb("KT", [128, S], BF16)
        VA = sb("VA", [128, NT, 2, 128], BF16)
        GFIN = sb("GFIN", [128, D], F32)
        GV = sb("GV", [128, 24], F32)
        C32 = sb("C32", [128, 384], F32)
        D3 = sb("D3", [128, 384], BF16)
        EPSC = sb("EPSC", [128, 4], F32)
        SM = sb("SM", [128, 256], F32)
        XS = sb("XS", [128, 2, D], F32)
        HT = sb("HT", [128, 2, D], F32)
        U32 = sb("U32", [128, D], F32)
        UTf = sb("UTf", [128, 8, 512], BF16)
        YS = UTf
        U2T = sb("U2T", [128, 8, 128], BF16)
        QT = sb("QT", [128, 2, 4, 512], BF16)
        GM = sb("GM", [128, 2, 8, 512], BF16)
        PT = sb("PT", [128, 3, 1024], BF16)
        RQ = sb("RQ", [128, 1024], F32)
        SQ = sb("SQ", [128, 512], F32)
        RS = sb("RS", [128, 512], F32)
        T1 = sb("T1", [128, 512], F32)
        T2 = sb("T2", [128, 512], F32)
        EG = sb("EG", [128, 512], F32)
        RSa = sb("RSa", [128, 512], F32)
        OAa = sb("OAa", [128, 512], F32)
        ET = sb("ET", [128, 512], F32)
        T2t = sb("T2t", [128, 512], F32)
        D1 = sb("D1", [128, 2, 384], BF16)
        PTL = sb("PTL", [128, 256], F32)
        PTT = sb("PTT", [128, 2, 128], BF16)
        PS = [es.enter_context(nc.psum_tensor(f"ps{i}", [128, 1024], F32)) for i in range(4)]
        sems = {n: es.enter_context(nc.semaphore(n)) for n in ["pe", "act", "dve", "pool"]}
        dsems = {}
        dcnt = {}

        def dsem(key):
            if key not in dsems:
                dsems[key] = es.enter_context(nc.semaphore("d_" + key))
                dcnt[key] = 0
            return dsems[key]

        es.enter_context(nc.Block())
        PE = Eng(nc.tensor, sems["pe"], "pe")
        ACT = Eng(nc.scalar, sems["act"], "act")
        DVE = Eng(nc.vector, sems["dve"], "dve")
        POOL = Eng(nc.gpsimd, sems["pool"], "pool")
        SP = Eng(nc.sync, None, "sp")
        COMPUTE = [PE, ACT, DVE, POOL]

        def dma(out, in_, key, waits=()):
            SP.wait(*waits)
            sem = dsem(key)
            nc.sync.dma_start(out=out, in_=in_).then_inc(sem, 16)
            dcnt[key] += 16
            return (key, sem, dcnt[key])

        def dtok(key):
            return (key, dsems[key], dcnt[key])

        def barrier(with_dma_keys=()):
            toks = [e.last() for e in COMPUTE]
            toks += [dtok(k) for k in with_dma_keys if k in dsems]
            for e in COMPUTE + [SP]:
                e.wait(*toks)

        def bank(b):
            return PS[b // 2][:, (b % 2) * 512:(b % 2) * 512 + 512]

        def bankb(b):
            return PS[b // 2][:, :].bitcast(BF16)[:, (b % 2) * 1024:(b % 2) * 1024 + 1024]

        bank_free = [None] * 8
        fbs = [6]

        p1_mode = [False]

        def fbank(owner=None):
            if p1_mode[0] and owner == "A":
                return 6
            if p1_mode[0] and owner == "tail":
                return 7
            b = fbs[0]
            fbs[0] = 13 - b
            return b

        smc = [0]

        def smcols(n):
            if smc[0] + n > 256:
                smc[0] = 0
            a = smc[0]
            smc[0] += n
            return SM[:, a:a + n]

        bdones = C32[:, 0:128]
        ccs = C32[0:64, 256:384]
        ident32 = C32[:, 128:256]
        identb = D3[:, 256:384]
        EPSA = EPSC[:, 0:1]
        MHALF = EPSC[:, 1:2]
        ONE = EPSC[:, 2:3]

        def delay(n):
            for _ in range(n):
                yield

        def run(gen):
            if gen is None:
                return
            for _ in gen:
                pass

        dma(GV[:], gvec[:, :], "c0")
        dma(GFIN[:], gfin[:, :], "c0")
        dma(C32[:], cst32[:, :], "c0")
        dma(D3[:], dft3[:, :], "c0")
        t_c = dtok("c0")
        nc.vector.memset(EPSC[:, 0:1], EPS)
        nc.vector.memset(EPSC[:, 1:2], -0.5)
        d_eps = DVE.done(nc.vector.memset(EPSC[:, 2:3], 1.0))
        p_ones = POOL.done(nc.gpsimd.memset(VA[:], 1.0))
        for e in COMPUTE:
            e.wait(t_c, d_eps, p_ones)
        stage = [XS[:, :, :].rearrange("p a c -> p (a c)"), HT[:, :, :].rearrange("p a c -> p (a c)")]
        stage_free = [None, None]
        sidx = [0]

        def prep(dst_ap, src_ap, ncols, gcol):
            i = sidx[0] % 2
            sidx[0] += 1
            st = stage[i][:, 0:ncols]
            t = dma(st, src_ap, f"st{i}", waits=[stage_free[i]])
            eng, e = (DVE, nc.vector)
            eng.wait(t)
            if gcol is None:
                d = eng.done(e.tensor_copy(out=dst_ap, in_=st))
            else:
                d = eng.done(e.tensor_scalar(out=dst_ap, in0=st, scalar1=gcol, scalar2=None, op0=ALU.mult))
            stage_free[i] = d
            return d

        for k in range(8):
            prep(W1[:, k, :], w1[k], 384, GV[:, k:k + 1])

        def prep_rest_g():
            for k in range(8):
                sidx[0] = 1
                prep(W2[:, k, :], w2[k], 2048, GV[:, k:k + 1])
                yield
                sidx[0] = 1
                prep(WO[:, k, :], wo[k], 1024, None)
                yield
                sidx[0] = 1
                prep(WPG[:, k, :], wpg[k], 1024, GV[:, 8 + k:9 + k])
                yield
            for k in range(2):
                sidx[0] = 1
                prep(WPLE[:, k, :], wple[k], 1024, 0.5)
                yield
        WFM = T1[0:64, :]
        t = dma(WFM, wfm.rearrange("c g d -> c (g d)"), "c1")
        PE.wait(t)
        for g in range(8):
            nc.tensor.matmul(PS[0][0:64, g * 128:g * 128 + 64], lhsT=ccs[:, 0:64], rhs=WFM[:, g * 64:(g + 1) * 64], start=True, stop=True)
            m = PE.done(nc.tensor.matmul(PS[0][0:64, g * 128 + 64:g * 128 + 128], lhsT=ccs[:, 64:128], rhs=WFM[:, g * 64:(g + 1) * 64], start=True, stop=True))
        DVE.wait(m)
        PRs = [T2[0:64, :], SQ[0:64, :]]
        DVE.done(nc.vector.tensor_copy(out=PRs[0], in_=PS[0][0:64, 0:512]))
        dpr = DVE.done(nc.vector.tensor_copy(out=PRs[1], in_=PS[0][0:64, 512:1024]))
        WFT = GM[:, :, :, :].rearrange("p a c n -> p (a c n)").bitcast(F32)[0:64, :]
        for hh in range(2):
            t = dma(WFT, wfT[:, hh * 4:(hh + 1) * 4, :].rearrange("c g f -> c (g f)"), "c2", waits=[PE.last()])
            PE.wait(t, dpr)
            for k in range(8):
                PE.wait(bank_free[2 + (k % 2)])
                for gg in range(4):
                    g = hh * 4 + gg
                    m = PE.done(nc.tensor.matmul(bank(2 + (k % 2))[:, gg * 128:(gg + 1) * 128],
                                                 lhsT=WFT[:, gg * 1024 + k * 128:gg * 1024 + (k + 1) * 128],
                                                 rhs=PRs[g // 4][:, (g % 4) * 128:(g % 4 + 1) * 128], start=True, stop=True))
                DVE.wait(m)
                src = bank(2 + (k % 2)).rearrange("p (g r d) -> p g r d", g=4, r=2)
                dst = W1F[:, k, :].rearrange("p (r g d) -> p g r d", r=2, g=8)[:, hh * 4:(hh + 1) * 4, :, :]
                d = DVE.done(nc.vector.tensor_scalar(out=dst, in0=src, scalar1=GV[:, k:k + 1], scalar2=None, op0=ALU.mult))
                bank_free[2 + (k % 2)] = d
        barrier(with_dma_keys=["c2", "st0", "st1"])

        u32_free = [None]

        def rstd_from_stats(x_ap, x_tok):
            st = smcols(12)
            mv = smcols(2)
            ms = smcols(1)
            ms2 = smcols(1)
            rstd = smcols(1)
            DVE.wait(x_tok)
            nc.vector.bn_stats(out=st[:, 0:6], in_=x_ap[:, 0:512])
            d = DVE.done(nc.vector.bn_stats(out=st[:, 6:12], in_=x_ap[:, 512:1024]))
            DVE.wait(d)
            d = DVE.done(nc.vector.bn_aggr(out=mv, in_=st))
            DVE.wait(d)
            d = DVE.done(nc.vector.scalar_tensor_tensor(out=ms, in0=mv[:, 0:1], scalar=mv[:, 0:1], in1=mv[:, 1:2], op0=ALU.mult, op1=ALU.add))
            DVE.wait(d)
            d = DVE.done(nc.vector.tensor_scalar(out=ms2, in0=ms, scalar1=EPS, scalar2=None, op0=ALU.add))
            POOL.wait(d)
            p = POOL.done(nc.gpsimd.tensor_tensor(out=rstd, in0=ms2, in1=MHALF, op=ALU.pow))
            return p, rstd

        u32alt_free = [None]

        def norm_transpose_g(x_ap, x_tok, ui, dst_ap, dst_free, owner=None, raw=False):
            p, rstd = rstd_from_stats(x_ap, x_tok)
            if raw:
                UB = x_ap
                p2 = x_tok
                ufree = [None]
                yield
            else:
                yield from delay(5)
                if p1_mode[0] and owner == "tail":
                    UB = QT[:, 0, :, :].rearrange("p a n -> p (a n)")[:, 0:1024]
                    ufree = u32alt_free
                else:
                    UB = U32[:].bitcast(BF16)[:, 0:1024]
                    ufree = u32_free
                DVE.wait(p, ufree[0])
                p2 = DVE.done(nc.vector.tensor_scalar(out=UB, in0=x_ap, scalar1=rstd, scalar2=None, op0=ALU.mult))
                yield from delay(4)
            d2 = None
            if raw:
                for h in range(2):
                    b = fbank(owner)
                    PE.wait(p2, bank_free[b])
                    for k in range(4):
                        kk = h * 4 + k
                        tr = nc.tensor.transpose(out=bank(b)[:, k * 128:(k + 1) * 128], in_=UB[:, kk * 128:(kk + 1) * 128], identity=ident32)
                    tp = PE.done(tr)
                    yield from delay(3)
                    DVE.wait(tp, dst_free)
                    d2 = DVE.done(nc.vector.tensor_copy(out=dst_ap[:, h * 4:(h + 1) * 4, :], in_=bank(b).rearrange("p (k n) -> p k n", k=4)))
                    bank_free[b] = d2
            else:
                b = fbank(owner)
                PE.wait(p2, bank_free[b])
                pb = bankb(b)
                for k in range(8):
                    tr = nc.tensor.transpose(out=pb[:, k * 128:(k + 1) * 128], in_=UB[:, k * 128:(k + 1) * 128], identity=identb)
                    if k == 3:
                        yield
                tp = PE.done(tr)
                yield from delay(3)
                DVE.wait(tp, dst_free)
                d2 = DVE.done(nc.vector.tensor_copy(out=dst_ap, in_=pb.rearrange("p (k n) -> p k n", k=8)))
                bank_free[b] = d2
            ufree[0] = tp
            if raw:
                return d2, tp, p, rstd
            return d2, p2

        ydone_keys = ["yo0", "yo1"]

        def phase1(slot, filler=None):
            var = 0 if slot < 2 else 1
            xfree = [None, None]
            utb = [UTf[:, :, 0:128], UTf[:, :, 128:256]]
            ut_free = [None, None]
            rk = [RQ[:, 0:256], RQ[:, 256:512]]
            rk_free = [None, None]
            d1_free = [None, None]
            Zb = PT[:, 0, :]
            Yb = PT[:, 1, :]
            pst = dict(z_free=None, y_free=None, t12_free=None)
            resA = {}

            xtok = {}
            for t0 in range(2):
                xtok[t0] = dma(XS[:, t0, :], xs[slot, t0], f"xs{t0}", waits=[xfree[t0]])

            def stageA(t):
                i = t % 2
                tx = xtok.pop(t)
                trk = dma(rk[i], ropek[t], f"rk{i}", waits=[rk_free[i]])
                td1 = dma(D1[:, i, :], dft1[var, t], f"d1{i}", waits=[d1_free[i]])
                u_tok, xrd = yield from norm_transpose_g(XS[:, i, :], tx, 0, utb[i], ut_free[i], owner="A")
                xfree[i] = xrd
                if t + 2 < NT:
                    xtok[t + 2] = dma(XS[:, i, :], xs[slot, t + 2], f"xs{i}", waits=[xrd])
                resA[t] = (u_tok, trk, td1)

            def stageB(t):
                i = t % 2
                u_tok, trk, td1 = resA.pop(t)
                uT = utb[i]
                PE.wait(u_tok, bank_free[0], bank_free[1], bank_free[2])
                for k in range(8):
                    nc.tensor.matmul(bank(0), lhsT=uT[:, k, :], rhs=W1F[:, k, 0:512], start=(k == 0), stop=(k == 7))
                    nc.tensor.matmul(bank(1), lhsT=uT[:, k, :], rhs=W1F[:, k, 512:1024], start=(k == 0), stop=(k == 7))
                    mz = nc.tensor.matmul(bank(2)[:, 0:128], lhsT=uT[:, k, :], rhs=W1[:, k, 256:384], start=(k == 0), stop=(k == 7))
                    if k % 2 == 1 and k < 7:
                        yield
                mz = PE.done(mz)
                yield
                PE.wait(bank_free[3])
                for k in range(8):
                    nc.tensor.matmul(bank(3)[:, 0:128], lhsT=W1[:, k, 0:128], rhs=uT[:, k, :], start=(k == 0), stop=(k == 7))
                yield
                for k in range(8):
                    mk = nc.tensor.matmul(bank(3)[:, 128:256], lhsT=W1[:, k, 128:256], rhs=uT[:, k, :], start=(k == 0), stop=(k == 7))
                mk = PE.done(mk)
                ut_free[i] = mk
                ACT.wait(mz, pst["z_free"])
                az = ACT.done(nc.scalar.copy(out=Zb, in_=PS[0][:, :]))
                bank_free[0] = az
                bank_free[1] = az
                DVE.wait(mz)
                nc.vector.tensor_copy(out=VA[:, t, 0, 0:64], in_=bank(2)[:, 0:64])
                dv = DVE.done(nc.vector.tensor_copy(out=VA[:, t, 1, 64:128], in_=bank(2)[:, 64:128]))
                yield
                bq, bqr = bank(3)[:, 0:128], bank(3)[:, 128:256]
                ACT.wait(mk, PE.last())
                a1 = ACT.done(nc.scalar.activation(out=SQ[:, 0:128], in_=bq, func=AF.Square))
                PE.wait(a1, dv)
                m = PE.done(nc.tensor.matmul(bank(2)[:, 0:128], lhsT=bdones, rhs=SQ[:, 0:128], start=True, stop=True))
                PE.wait(az, td1, bank_free[4], bank_free[5])
                nc.tensor.matmul(bank(4), lhsT=D1[:, i, 0:128], rhs=Zb[:, 0:512], start=True, stop=False)
                nc.tensor.matmul(bank(4), lhsT=D1[:, i, 128:256], rhs=Zb[:, 512:1024], start=False, stop=True)
                nc.tensor.matmul(bank(5), lhsT=D1[:, i, 0:128], rhs=Zb[:, 512:1024], start=True, stop=False)
                my = PE.done(nc.tensor.matmul(bank(5), lhsT=D1[:, i, 256:384], rhs=Zb[:, 0:512], start=False, stop=True))
                pst["z_free"] = my
                d1_free[i] = my
                yield
                ACT.wait(m, pst["t12_free"])
                a2 = ACT.done(nc.scalar.activation(out=RS[:, 0:128], in_=bank(2)[:, 0:128], func=AF.Sqrt, scale=1.0 / 64, bias=EPSA))
                bank_free[2] = a2
                DVE.wait(mk, trk, pst["t12_free"], a1)
                d1 = DVE.done(nc.vector.scalar_tensor_tensor(out=T1[:, 0:128], in0=bq, scalar=GV[:, 18:19], in1=rk[i][:, 0:128], op0=ALU.mult, op1=ALU.mult))
                d2 = DVE.done(nc.vector.scalar_tensor_tensor(out=T2[:, 0:128], in0=bqr, scalar=GV[:, 19:20], in1=rk[i][:, 128:256], op0=ALU.mult, op1=ALU.mult))
                bank_free[3] = [a1, d2]
                rk_free[i] = d2
                DVE.wait(a2)
                a3 = DVE.done(nc.vector.reciprocal(out=RS[:, 0:128], in_=RS[:, 0:128]))
                ACT.wait(my, pst["y_free"])
                dy = ACT.done(nc.scalar.copy(out=Yb, in_=PS[2][:, :]))
                bank_free[4] = dy
                bank_free[5] = dy
                yield
                POOL.wait(d1, d2)
                p1 = POOL.done(nc.gpsimd.tensor_tensor(out=T1[:, 0:128], in0=T1[:, 0:128], in1=T2[:, 0:128], op=ALU.add))
                POOL.wait(p1, a3)
                p2 = POOL.done(nc.gpsimd.tensor_tensor(out=KT[:, t * 128:(t + 1) * 128], in0=T1[:, 0:128], in1=RS[:, 0:128], op=ALU.mult))
                pst["t12_free"] = p2
                ydst = ysc[slot].rearrange("r g p c -> g r p c")
                for b_lo in range(2):
                    for alo in range(2):
                        p0 = b_lo * 64 + alo * 32
                        ty = dma(ydst[:, :, alo * 64 + 2 * t + b_lo, :], Yb[p0:p0 + 32, :].rearrange("p (r c) -> p r c", r=2),
                                 "ysc", waits=[dy])
                pst["y_free"] = ty
                yield

            p1_mode[0] = filler is not None
            run(stageA(0))
            for t in range(NT):
                gb = stageB(t)
                ga = stageA(t + 1) if t + 1 < NT else None
                while gb is not None or ga is not None:
                    if gb is not None:
                        try:
                            next(gb)
                        except StopIteration:
                            gb = None
                    if ga is not None:
                        try:
                            next(ga)
                        except StopIteration:
                            ga = None
                    if filler is not None:
                        try:
                            next(filler)
                        except StopIteration:
                            filler = None
            run(filler)
            p1_mode[0] = False

        st = dict(rq_free=None, xs_free=[None, None], utf_free=None, sq_free=None, rs_free=None, t1_free=None, t2_free=None,
                  eg_free=None, ys_free=None, oa_free=None, ht_free=[None, None], ptl_free=None,
                  ptt_free=None, et_free=None, t2t_free=None, u2t_free=None)
        qt_ready = [[None] * 4, [None] * 4]
        qt_free = [None, None]
        gm_ready = [[None] * 8, [None] * 8]
        gm_free = [None, None]
        am_ready = [[None] * 4, [None] * 4]

        def proj_g(cc, u_tok):
            b = fbank()
            PE.wait(u_tok, bank_free[b])
            for k in range(8):
                mm = nc.tensor.matmul(bank(b), lhsT=W2[:, k, cc * 128:(cc + 1) * 128], rhs=UTf[:, k, :], start=(k == 0), stop=(k == 7))
                if k == 3:
                    yield
            return b, PE.done(mm)

        def front_prefetch(qt):
            trq = dma(RQ[:], ropeq[qt], "rq", waits=[st["rq_free"]])
            txs = {}
            for sub in range(2):
                txs[sub] = dma(XS[:, sub, :], xq[qt, sub * 128:(sub + 1) * 128, :], f"xs{sub}", waits=[st["xs_free"][sub]])
            return trq, txs

        def front_g(slot, j, qt, qi, pre=None):
            if pre is None:
                pre = front_prefetch(qt)
                yield from delay(3)
            trq, txs = pre
            u_tok = None
            for sub in range(4):
                i = sub % 2
                u_tok, xrd = yield from norm_transpose_g(XS[:, i, :], txs[sub], 0, UTf[:, :, sub * 128:(sub + 1) * 128],
                                                         st["utf_free"] if sub == 0 else None)
                st["xs_free"][i] = xrd
                if sub + 2 < 4:
                    txs[sub + 2] = dma(XS[:, i, :], xq[qt, (sub + 2) * 128:(sub + 3) * 128, :], f"xs{i}", waits=[xrd])
                yield
            yield from delay(2)
            for p in range(4):
                bq, mq = yield from proj_g(p, u_tok)
                yield from delay(2)
                ACT.wait(mq, st["sq_free"])
                a1 = ACT.done(nc.scalar.activation(out=SQ[:], in_=bank(bq), func=AF.Square))
                bs = fbank()
                PE.wait(a1, bank_free[bs])
                m = PE.done(nc.tensor.matmul(bank(bs), lhsT=bdones, rhs=SQ[:], start=True, stop=True))
                st["sq_free"] = m
                DVE.wait(mq, trq, st["t1_free"], a1)
                d1 = DVE.done(nc.vector.scalar_tensor_tensor(out=T1[:], in0=bank(bq), scalar=GV[:, 16:17], in1=RQ[:, 0:512], op0=ALU.mult, op1=ALU.mult))
                bank_free[bq] = [a1, d1]
                yield from delay(2)
                ACT.wait(m, st["rs_free"])
                a2 = ACT.done(nc.scalar.activation(out=RS[:], in_=bank(bs), func=AF.Sqrt, scale=1.0 / 64, bias=EPSA))
                bank_free[bs] = a2
                DVE.wait(a2)
                a3 = DVE.done(nc.vector.reciprocal(out=RS[:], in_=RS[:]))
                br, mr = yield from proj_g(4 + p, u_tok)
                yield from delay(2)
                DVE.wait(mr, st["t2_free"])
                d2 = DVE.done(nc.vector.scalar_tensor_tensor(out=T2[:], in0=bank(br), scalar=GV[:, 17:18], in1=RQ[:, 512:1024], op0=ALU.mult, op1=ALU.mult))
                bank_free[br] = d2
                POOL.wait(d1, d2)
                p1 = POOL.done(nc.gpsimd.tensor_tensor(out=T1[:], in0=T1[:], in1=T2[:], op=ALU.add))
                st["t2_free"] = p1
                POOL.wait(p1, a3, qt_free[qi])
                p2 = POOL.done(nc.gpsimd.tensor_tensor(out=QT[:, qi, p, :], in0=T1[:], in1=RS[:], op=ALU.mult))
                st["t1_free"] = p2
                st["rs_free"] = p2
                qt_ready[qi][p] = p2
                st["rq_free"] = [d1, d2]
                yield
            for c in range(8):
                bg, mg = yield from proj_g(8 + c, u_tok)
                if c == 7:
                    st["utf_free"] = mg
                    for r in range(2):
                        for al in range(4):
                            tys = dma(YS[:, r * 4 + al, :], ysc[slot, r, 4 * j + al, :, :], "ysb", waits=[st["ys_free"], st["utf_free"]])
                yield from delay(2)
                ACT.wait(mg, st["eg_free"])
                a1 = ACT.done(nc.scalar.activation(out=EG[:], in_=bank(bg), func=AF.Tanh, scale=0.5))
                yield
                DVE.wait(a1, gm_free[qi])
                pg = DVE.done(nc.vector.scalar_tensor_tensor(out=GM[:, qi, c, :], in0=EG[:], scalar=1.0, in1=bank(bg), op0=ALU.add, op1=ALU.mult))
                bank_free[bg] = pg
                st["eg_free"] = pg
                gm_ready[qi][c] = pg
            yield from delay(4)
            for c in range(4):
                b = fbank()
                PE.wait(tys, bank_free[b])
                for al in range(4):
                    nc.tensor.matmul(bank(b)[:, al * 128:(al + 1) * 128], lhsT=YS[:, al, c * 128:(c + 1) * 128], rhs=D3[:, 0:128], start=True, stop=False)
                    mf = nc.tensor.matmul(bank(b)[:, al * 128:(al + 1) * 128], lhsT=YS[:, 4 + al, c * 128:(c + 1) * 128], rhs=D3[:, 128:256], start=False, stop=True)
                    if al == 1:
                        yield
                mf = PE.done(mf)
                yield from delay(2)
                DVE.wait(mf, gm_ready[qi][4 + c])
                df = DVE.done(nc.vector.scalar_tensor_tensor(out=GM[:, qi, 4 + c, :], in0=bank(b), scalar=0.5, in1=GM[:, qi, 4 + c, :], op0=ALU.mult, op1=ALU.mult))
                bank_free[b] = df
                gm_ready[qi][4 + c] = df
            st["ys_free"] = mf
            st["utf_free"] = mf

        def tail_g(slot, j, qt, qi):
            txs = {}
            tps = {}
            for sub in range(2):
                txs[sub] = dma(HT[:, sub, :], xq[qt, sub * 128:(sub + 1) * 128, :], f"ht{sub}", waits=[st["ht_free"][sub]])
            tps[0] = dma(PTL[:], pq[qt, 0:128, :], "ptl", waits=[st["ptl_free"]])
            yield from delay(2)
            for sub in range(4):
                i = sub % 2
                tsl = slice(sub * 128, (sub + 1) * 128)
                tx = txs[sub]
                tp_ = tps[sub]
                for half in range(2):
                    hs = slice(half * 512, (half + 1) * 512)
                    b = fbank("tail")
                    PE.wait(am_ready[qi], gm_ready[qi][4:8], bank_free[b])
                    for kc in range(8):
                        mm = nc.tensor.matmul(bank(b), lhsT=GM[:, qi, kc, tsl], rhs=WO[:, kc, hs], start=(kc == 0), stop=(kc == 7))
                        if kc == 3:
                            yield
                    mm = PE.done(mm)
                    yield from delay(2)
                    DVE.wait(mm, tx)
                    dh = DVE.done(nc.vector.tensor_tensor(out=HT[:, i, hs], in0=bank(b), in1=HT[:, i, hs], op=ALU.add))
                    bank_free[b] = dh
                if sub == 3:
                    gm_free[qi] = mm
                u2_tok, tr_tok, pr2, rstd2 = yield from norm_transpose_g(HT[:, i, :], dh, 1, U2T[:, :, :], st["u2t_free"], owner="tail", raw=True)
                hsc = smcols(1)
                DVE.wait(pr2)
                dsc = DVE.done(nc.vector.tensor_scalar(out=hsc, in0=rstd2, scalar1=0.5, scalar2=None, op0=ALU.mult))
                b = fbank("tail")
                PE.wait(tp_, bank_free[b])
                nc.tensor.transpose(out=bank(b)[:, 0:128], in_=PTL[:, 0:128], identity=ident32)
                mp = PE.done(nc.tensor.transpose(out=bank(b)[:, 128:256], in_=PTL[:, 128:256], identity=ident32))
                st["ptl_free"] = mp
                if sub + 1 < 4:
                    tps[sub + 1] = dma(PTL[:], pq[qt, (sub + 1) * 128:(sub + 2) * 128, :], "ptl", waits=[mp])
                yield from delay(2)
                DVE.wait(mp, st["ptt_free"])
                dp = DVE.done(nc.vector.tensor_copy(out=PTT[:, :, :], in_=bank(b)[:, 0:256].rearrange("p (k n) -> p k n", k=2)))
                bank_free[b] = dp
                for half in range(2):
                    hs = slice(half * 512, (half + 1) * 512)
                    bg = fbank("tail")
                    PE.wait(u2_tok, bank_free[bg])
                    for k in range(8):
                        mg = nc.tensor.matmul(bank(bg), lhsT=U2T[:, k, :], rhs=WPG[:, k, hs], start=(k == 0), stop=(k == 7))
                        if k == 3:
                            yield
                    mg = PE.done(mg)
                    yield from delay(2)
                    ACT.wait(mg, st["et_free"], dsc)
                    a1 = ACT.done(nc.scalar.activation(out=ET[:], in_=bank(bg), func=AF.Tanh, scale=hsc))
                    bank_free[bg] = a1
                    bp = fbank("tail")
                    PE.wait(dp, bank_free[bp])
                    nc.tensor.matmul(bank(bp), lhsT=PTT[:, 0, :], rhs=WPLE[:, 0, hs], start=True, stop=False)
                    mpl = PE.done(nc.tensor.matmul(bank(bp), lhsT=PTT[:, 1, :], rhs=WPLE[:, 1, hs], start=False, stop=True))
                    yield from delay(2)
                    DVE.wait(a1, mpl, st["t2t_free"])
                    dt_ = DVE.done(nc.vector.scalar_tensor_tensor(out=T2t[:], in0=ET[:], scalar=1.0, in1=bank(bp), op0=ALU.add, op1=ALU.mult))
                    bank_free[bp] = dt_
                    st["et_free"] = dt_
                    POOL.wait(dt_, tr_tok)
                    ph = POOL.done(nc.gpsimd.tensor_tensor(out=HT[:, i, hs], in0=HT[:, i, hs], in1=T2t[:], op=ALU.add))
                    st["t2t_free"] = ph
                st["u2t_free"] = mg
                st["ptt_free"] = mpl
                pr, rstd = rstd_from_stats(HT[:, i, :], ph)
                yield
                DVE.wait(pr)
                dyo = DVE.done(nc.vector.scalar_tensor_tensor(out=HT[:, i, :], in0=HT[:, i, :], scalar=rstd, in1=GFIN[:], op0=ALU.mult, op1=ALU.mult))
                st["ht_free"][i] = dma(y[qt, tsl, :], HT[:, i, :], ydone_keys[i], waits=[dyo])
                if sub + 2 < 4:
                    txs[sub + 2] = dma(HT[:, i, :], xq[qt, (sub + 2) * 128:(sub + 3) * 128, :], f"ht{i}", waits=[st["ht_free"][i]])
                yield

        pt_free = [None, None, None]

        def attention(qi, filler, rate_num=5, rate_den=2):
            n_it = 4 * NT
            ms = {}

            def qk(it):
                p, kt = divmod(it, NT)
                si = it % 2
                PE.wait(qt_ready[qi][p], bank_free[2 * si], bank_free[2 * si + 1])
                nc.tensor.matmul(bank(2 * si), lhsT=KT[0:64, kt * 128:(kt + 1) * 128], rhs=QT[0:64, qi, p, :], start=True, stop=True)
                return PE.done(nc.tensor.matmul(bank(2 * si + 1), lhsT=KT[64:128, kt * 128:(kt + 1) * 128], rhs=QT[64:128, qi, p, :], start=True, stop=True))

            ms[0] = qk(0)
            ms[1] = qk(1)
            acc = 0
            nitems = [0]
            fin_it = [None]
            for it in range(n_it):
                p, kt = divmod(it, NT)
                si = it % 2
                pi = it % 3
                ACT.wait(ms.pop(it), pt_free[pi])
                ae = ACT.done(nc.scalar.activation(out=PT[:, pi, :], in_=PS[si][:, :], func=AF.Exp, scale=0.125))
                bank_free[2 * si] = ae
                bank_free[2 * si + 1] = ae
                if it + 2 < n_it:
                    ms[it + 2] = qk(it + 2)
                else:
                    qt_free[qi] = PE.last()
                if kt == 0:
                    PE.wait(bank_free[4], bank_free[5])
                PE.wait(ae)
                nc.tensor.matmul(bank(4), lhsT=VA[:, kt, 0, :], rhs=PT[:, pi, 0:512], start=(kt == 0), stop=(kt == NT - 1))
                mo = PE.done(nc.tensor.matmul(bank(5), lhsT=VA[:, kt, 1, :], rhs=PT[:, pi, 512:1024], start=(kt == 0), stop=(kt == NT - 1)))
                pt_free[pi] = mo
                if kt == NT - 1:
                    DVE.wait(mo, st["oa_free"])
                    nc.vector.tensor_copy(out=RSa[0:64, :], in_=bank(4)[64:128, :])
                    nc.vector.tensor_copy(out=RSa[64:128, :], in_=bank(5)[0:64, :])
                    nc.vector.tensor_copy(out=OAa[0:64, :], in_=bank(4)[0:64, :])
                    dc = DVE.done(nc.vector.tensor_copy(out=OAa[64:128, :], in_=bank(5)[64:128, :]))
                    bank_free[4] = dc
                    bank_free[5] = dc
                    DVE.wait(dc)
                    dr = DVE.done(nc.vector.reciprocal(out=RSa[:], in_=RSa[:]))
                    DVE.wait(dr)
                    dn = DVE.done(nc.vector.scalar_tensor_tensor(out=OAa[:], in0=OAa[:], scalar=0.5, in1=RSa[:], op0=ALU.mult, op1=ALU.mult))
                    POOL.wait(dn, gm_ready[qi][p])
                    pg = POOL.done(nc.gpsimd.tensor_tensor(out=GM[:, qi, p, :], in0=OAa[:], in1=GM[:, qi, p, :], op=ALU.mult))
                    st["oa_free"] = pg
                    am_ready[qi][p] = pg
                if filler is not None:
                    acc += rate_num
                    while acc >= rate_den:
                        acc -= rate_den
                        try:
                            next(filler)
                            nitems[0] += 1
                        except StopIteration:
                            filler = None
                            fin_it[0] = it
                            break
            left = 0
            if filler is not None:
                for _ in filler:
                    left += 1

        def chain(*gens):
            for g in gens:
                if g is not None:
                    yield from g

        qt_base = 0
        gq = 0
        pending_tail = None
        for slot in range(n_slots):
            if slot == 0:
                barrier()
            else:
                barrier(with_dma_keys=["ysb"])
            phase1(slot, pending_tail if slot > 0 else prep_rest_g())
            pending_tail = None
            barrier(with_dma_keys=["ysc", "st1"])
            nq = slot_qt[slot]
            if nq == 0:
                qt_base += SLOT_QT[slot]
                continue
            run(front_g(slot, 0, qt_base, gq % 2))
            for j in range(nq):
                qi = (gq + j) % 2
                fl = []
                if j > 0:
                    fl.append(tail_g(slot, j - 1, qt_base + j - 1, (gq + j - 1) % 2))
                if j + 1 < nq:
                    pre = front_prefetch(qt_base + j + 1)
                    fl.append(front_g(slot, j + 1, qt_base + j + 1, (gq + j + 1) % 2, pre))
                attention(qi, chain(*fl) if fl else None)
            pending_tail = tail_g(slot, nq - 1, qt_base + nq - 1, (gq + nq - 1) % 2)
            if slot == n_slots - 1:
                run(pending_tail)
            gq += nq
            qt_base += SLOT_QT[slot]
        for k in ydone_keys:
            if k in dsems:
                SP.wait(dtok(k))
                for e in COMPUTE:
                    e.wait(dtok(k))
    return nc


def _rope_tables():
    f32 = np.float32
    inv = (f32(10000.0) ** (-(np.arange(16, dtype=f32)) / f32(16))).astype(f32)
    s = np.arange(S)
    row = (s // 64).astype(f32)
    col = (s % 64).astype(f32)
    ar = row[:, None] * inv[None, :]
    ac = col[:, None] * inv[None, :]
    ang = np.concatenate([ar, ar, ac, ac], axis=-1).astype(f32)
    cos = np.cos(ang).astype(f32)
    sin = np.sin(ang).astype(f32)
    sign = np.ones(64, f32)
    sign[0:16] = -1
    sign[32:48] = -1
    return cos, sin * sign[None, :]


PERM64 = np.concatenate([np.arange(16, 32), np.arange(0, 16), np.arange(48, 64), np.arange(32, 48)])


def _p1_tokens():
    t = np.arange(NT)[:, None]
    p = np.arange(128)[None, :]
    return 64 * (p % 64) + 2 * t + (p // 64)


def _qt_tokens(j):
    n = np.arange(512)
    al, alo, bp = n // 128, (n // 64) % 2, n % 64
    return (8 * j + 2 * al + alo) + 64 * bp


def _dft_tables(half):
    out = np.zeros((NT, 128, 3, 128), np.float64)
    a = np.arange(64)
    mm = np.arange(64)
    alo, g = mm // 32, mm % 32
    ap = (2 * g + alo + 32 * half) % 64
    for t in range(NT):
        for b_lo in range(2):
            b = 2 * t + b_lo
            th = 2 * np.pi * (np.outer(a, ap) / 64.0 + (b * ap)[None, :] / 4096.0)
            sl = slice(b_lo * 64, b_lo * 64 + 64)
            out[t, sl, 0, sl] = np.cos(th)
            out[t, sl, 1, sl] = np.sin(th)
            out[t, sl, 2, sl] = -np.sin(th)
    return out.reshape(NT, 128, 384)


_CONST_CACHE = {}


def _consts():
    if _CONST_CACHE:
        return _CONST_CACHE
    cos, sins = _rope_tables()
    c = _CONST_CACHE
    c["cos"], c["sins"] = cos, sins
    bd = np.zeros((128, 128), np.float32)
    bd[0:64, 0:64] = 1
    bd[64:128, 64:128] = 1
    cc = np.arange(64)
    th = 2 * np.pi * np.outer(cc, cc) / 64.0
    cst = np.zeros((128, 384), np.float32)
    cst[:, 0:128] = bd
    cst[:, 128:256] = np.eye(128, dtype=np.float32)
    cst[0:64, 256:320] = (np.cos(th) / 512.0).astype(np.float32)
    cst[0:64, 320:384] = (-np.sin(th) / 512.0).astype(np.float32)
    c["cst32"] = cst
    d3 = np.zeros((128, 384), np.float64)
    d3[:, 256:384] = np.eye(128)
    for alo in range(2):
        sl = slice(alo * 64, alo * 64 + 64)
        d3[sl, 0:128][:, sl] = np.cos(th)
        d3[sl, 128:256][:, sl] = np.sin(th)
    c["dft3"] = d3.astype(ml_dtypes.bfloat16)
    c["dft1"] = [_dft_tables(0).astype(ml_dtypes.bfloat16), _dft_tables(1).astype(ml_dtypes.bfloat16)]
    tok1 = _p1_tokens()
    rk = np.zeros((NT, 128, 256), np.float32)
    for t in range(NT):
        ct = cos[tok1[t]].T
        st = sins[tok1[t]].T
        rk[t, 0:64, 0:128] = ct
        rk[t, 64:128, 0:128] = ct
        rk[t, 0:64, 128:256] = st
        rk[t, 64:128, 128:256] = st
    c["ropek"] = rk
    rq = np.zeros((8, 128, 1024), np.float32)
    for j in range(8):
        tk = _qt_tokens(j)
        ct = cos[tk].T
        st = sins[tk].T
        rq[j, 0:64, 0:512] = ct
        rq[j, 64:128, 0:512] = ct
        rq[j, 0:64, 512:1024] = st
        rq[j, 64:128, 512:1024] = st
    c["ropeq8"] = rq
    c["tok1"] = tok1
    c["qtok"] = np.stack([_qt_tokens(j) for j in range(8)])
    return c


def _weights(g_norm, w_in, g_q, g_k, w_fmix, w_out, g_ple, w_ple_gate, w_ple, g_final):
    w_in = np.asarray(w_in[0], np.float32)
    q, k, v = w_in[:, 0:512], w_in[:, 512:640], w_in[:, 640:768]
    ga, f, gf = w_in[:, 768:1280], w_in[:, 1280:1792], w_in[:, 1792:2304]
    pair_cols = np.concatenate([np.concatenate([np.arange(p * 64, p * 64 + 64), np.arange((p + 4) * 64, (p + 4) * 64 + 64)]) for p in range(4)])
    rot_cols512 = (np.arange(512) // 64) * 64 + PERM64[np.arange(512) % 64]
    rot_cols128 = (np.arange(128) // 64) * 64 + PERM64[np.arange(128) % 64]
    qrot = q[:, rot_cols512]
    krot = k[:, rot_cols128]
    w1 = np.concatenate([k, krot, v], axis=1).reshape(8, 128, 384)
    w2 = np.concatenate([q[:, pair_cols], qrot[:, pair_cols], ga[:, pair_cols], gf], axis=1).reshape(8, 128, 2048)
    wfT = np.ascontiguousarray(f.T.reshape(8, 64, D).transpose(1, 0, 2))
    wo_rows = np.concatenate([pair_cols, np.arange(512, 1024)])
    wo = np.asarray(w_out[0], np.float32)[wo_rows].reshape(8, 128, D)
    wpg = np.asarray(w_ple_gate[0], np.float32).reshape(8, 128, D)
    wple = np.asarray(w_ple[0], np.float32).reshape(2, 128, D)
    wfm = np.ascontiguousarray(np.asarray(w_fmix[0], np.float32).transpose(1, 0, 2))
    gv = np.zeros((128, 24), np.float32)
    gv[:, 0:8] = np.asarray(g_norm[0], np.float32).reshape(8, 128).T
    gv[:, 8:16] = np.asarray(g_ple[0], np.float32).reshape(8, 128).T
    gq = np.asarray(g_q[0], np.float32)
    gk = np.asarray(g_k[0], np.float32)
    gv[:, 16] = np.tile(gq, 2)
    gv[:, 17] = np.tile(gq[PERM64], 2)
    gv[:, 18] = np.tile(gk, 2)
    gv[:, 19] = np.tile(gk[PERM64], 2)
    gfin = np.ascontiguousarray(np.broadcast_to(np.asarray(g_final, np.float32)[None, :], (128, D)))
    return dict(w1=np.ascontiguousarray(w1), w2=np.ascontiguousarray(w2), wfT=wfT, wo=np.ascontiguousarray(wo),
                wpg=np.ascontiguousarray(wpg), wple=np.ascontiguousarray(wple), wfm=wfm, gvec=gv, gfin=gfin)


def _core_plan(c):
    return [(2 * c, None), (2 * c + 1, None), (16 + c // 2, c % 2)]


def _prepare(x_prompt, x_sample, p_prompt, p_sample, g_norm, w_in, g_q, g_k, w_fmix, w_out,
             g_ple, w_ple_gate, w_ple, g_final, cores=range(NCORES)):
    cst = _consts()
    x_prompt = np.asarray(x_prompt, np.float32)
    x_sample = np.asarray(x_sample, np.float32)
    p_prompt = np.asarray(p_prompt, np.float32)[0]
    p_sample = np.asarray(p_sample, np.float32)[0]
    nb_p = x_prompt.shape[0]

    def xseq(i):
        return x_prompt[i] if i < nb_p else x_sample[i - nb_p]

    def pseq(i):
        return p_prompt[i] if i < nb_p else p_sample[i - nb_p]

    wts = _weights(g_norm, w_in, g_q, g_k, w_fmix, w_out, g_ple, w_ple_gate, w_ple, g_final)
    tok1 = cst["tok1"].reshape(-1)
    in_maps = []
    plans = []
    for c in cores:
        plan = _core_plan(c)
        xs = np.empty((3, NT, 128, D), np.float32)
        xq = np.empty((NQT, 512, D), np.float32)
        pq = np.empty((NQT, 512, 256), np.float32)
        rq = np.empty((NQT, 128, 1024), np.float32)
        qmap = []
        qi = 0
        for slot, (si, half) in enumerate(plan):
            xx = xseq(si)
            pp = pseq(si)
            xs[slot] = xx[tok1].reshape(NT, 128, D)
            js = range(8) if half is None else range(4 * half, 4 * half + 4)
            for j in js:
                tk = cst["qtok"][j]
                xq[qi] = xx[tk]
                pq[qi] = pp[tk]
                rq[qi] = cst["ropeq8"][j]
                qmap.append((si, j))
                qi += 1
        d1 = np.stack([cst["dft1"][0], cst["dft1"][plan[2][1]]])
        m = dict(xs=xs, xq=xq, pq=pq, ropeq=rq, dft1=d1, dft3=cst["dft3"], cst32=cst["cst32"], ropek=cst["ropek"])
        m.update(wts)
        in_maps.append(m)
        plans.append(qmap)
    return in_maps, plans, nb_p


def kernel(x_prompt, x_sample, p_prompt, p_sample, g_norm, w_in, g_q, g_k, w_fmix, w_out,
           g_ple, w_ple_gate, w_ple, g_final):
    cst = _consts()
    in_maps, plans, nb_p = _prepare(x_prompt, x_sample, p_prompt, p_sample, g_norm, w_in, g_q, g_k, w_fmix, w_out,
                                    g_ple, w_ple_gate, w_ple, g_final)
    nc = build_nc()
    res = run_bass_kernel_spmd(nc, in_maps, core_ids=list(range(NCORES)))
    y_all = np.empty((20, S, D), np.float32)
    for c in range(NCORES):
        yc = res.results[c]["y"]
        for qi, (si, j) in enumerate(plans[c]):
            y_all[si][cst["qtok"][j]] = yc[qi]
    return (np.ascontiguousarray(y_all[:nb_p]), np.ascontiguousarray(y_all[nb_p:]))
```
